# Optimizing a Trainium2 kernel written in Bass

```python
import math
import jax, jax.numpy as jnp
from jax import lax
import numpy as np

D_MODEL = 2048
BATCH = 4
SEQ = 4096
DEPTH = 1
DEC_BATCH = 8
DEC_SEQ = 2048
PAST_LEN = 128

GRID_W = 64
NA_HEADS = 8
NA_HEAD_DIM = 128
NA_WIDTH = NA_HEADS * NA_HEAD_DIM
WIN_H = 8
WIN_W = 16
Q_CB = 16
K_CB = Q_CB + WIN_W
N_CB = GRID_W // Q_CB
RW_HEAD = 64
RW_WIDTH = D_MODEL // 2
RW_HEADS = RW_WIDTH // RW_HEAD
DECAY_LORA = 64
ICLR_LORA = 64
GATE_LORA = 160
RW_IN = 3 * RW_WIDTH + 2 * DECAY_LORA + 2 * ICLR_LORA + GATE_LORA
IN_WIDTH = 3 * NA_WIDTH + RW_IN + 2 * D_MODEL
FFN_DIM = 5632
NORM_EPS = 1e-6
GN_EPS = 64e-5
L2_EPS = 1e-12

kernel_name = 'hybrid_na_rwkv7_convffn_encoder'


def rms_norm(x, g):
    xf = x.astype(jnp.float32)
    y = xf * lax.rsqrt(jnp.mean(xf * xf, axis=-1, keepdims=True) + NORM_EPS)
    return (y * g.astype(jnp.float32)).astype(x.dtype)


def shift_prev(z):
    return jnp.pad(z, ((0, 0), (1, 0), (0, 0)))[:, :-1]


def shift_next(z):
    return jnp.pad(z, ((0, 0), (0, 1), (0, 0)))[:, 1:]


def neighborhood_attention(q, k, v, rpb):
    B, T, H, hd = q.shape
    rows = T // GRID_W
    kh = min(WIN_H, rows)
    qg = q.reshape(B, rows, GRID_W, H, hd)
    cols = np.arange(GRID_W)
    col_start = np.clip(cols - WIN_W // 2, 0, GRID_W - WIN_W).reshape(N_CB, Q_CB)
    blk_start = np.clip(np.arange(N_CB) * Q_CB - WIN_W // 2, 0, GRID_W - K_CB)
    key_cols = blk_start[:, None] + np.arange(K_CB)[None, :]
    q_cols = cols.reshape(N_CB, Q_CB)
    dx = key_cols[:, None, :] - q_cols[:, :, None]
    col_valid = jnp.asarray((key_cols[:, None, :] >= col_start[:, :, None])
                            & (key_cols[:, None, :] < col_start[:, :, None] + WIN_W))
    dx_idx = jnp.asarray(np.clip(dx + WIN_W - 1, 0, 2 * WIN_W - 2))
    kc = k.reshape(B, rows, GRID_W, H, hd)[:, :, key_cols]
    vc = v.reshape(B, rows, GRID_W, H, hd)[:, :, key_cols]
    scale = NA_HEAD_DIM ** -0.5

    def attend_row(r):
        rs = jnp.clip(r - kh // 2, 0, rows - kh)
        k_blk = lax.dynamic_slice_in_dim(kc, rs, kh, axis=1)
        v_blk = lax.dynamic_slice_in_dim(vc, rs, kh, axis=1)
        q_row = lax.dynamic_index_in_dim(qg, r, axis=1, keepdims=False).reshape(B, N_CB, Q_CB, H, hd)
        s = jnp.einsum('bnqhd,bynkhd->bhnqyk', q_row, k_blk,
                       preferred_element_type=jnp.float32) * scale
        dy_idx = rs + jnp.arange(kh) - r + (WIN_H - 1)
        bias = rpb[:, dy_idx][:, :, dx_idx].transpose(0, 2, 3, 1, 4).astype(jnp.float32)
        s = jnp.where(col_valid[:, :, None, :], s + bias, -jnp.inf)
        p = jax.nn.softmax(s.reshape(B, H, N_CB, Q_CB, kh * K_CB), axis=-1).reshape(s.shape)
        o = jnp.einsum('bhnqyk,bynkhd->bnqhd', p.astype(v.dtype), v_blk)
        return o.reshape(B, GRID_W, H * hd)

    out = lax.map(attend_row, jnp.arange(rows))
    return jnp.moveaxis(out, 0, 1).reshape(B, T, H * hd)


def rwkv7_step(S, inp):
    r, w, k, a_vec, b_vec, v = inp
    sa = jnp.einsum('dbhvk,dbhk->dbhv', S, a_vec)
    S = S * w[..., None, :] + sa[..., :, None] * b_vec[..., None, :] + v[..., :, None] * k[..., None, :]
    y = jnp.einsum('dbhvk,dbhk->dbhv', S, r)
    return S, y


def rwkv7_bidirectional(z, w0, w_up, a0, a_up, g_up, k_k, k_a, r_k, gn_w, gn_b):
    B, T, _ = z.shape
    C, H, N = RW_WIDTH, RW_HEADS, RW_HEAD
    z = z.astype(jnp.float32)
    r, k, v = z[..., :C], z[..., C:2 * C], z[..., 2 * C:3 * C]
    o = 3 * C
    dw = z[..., o:o + 2 * DECAY_LORA].reshape(B, T, 2, DECAY_LORA)
    o += 2 * DECAY_LORA
    da = z[..., o:o + 2 * ICLR_LORA].reshape(B, T, 2, ICLR_LORA)
    o += 2 * ICLR_LORA
    dg = z[..., o:o + GATE_LORA]
    w_pre = jnp.einsum('btdl,dlc->dbtc', jnp.tanh(dw), w_up) + w0[:, None, None, :]
    decay = jnp.exp(-jnp.exp(-jax.nn.softplus(-w_pre) - 0.5))
    a = jax.nn.sigmoid(jnp.einsum('btdl,dlc->dbtc', da, a_up) + a0[:, None, None, :])
    g = jax.nn.sigmoid(dg) @ g_up
    kk = (k * k_k).reshape(B, T, H, N)
    kk = (kk * lax.rsqrt(jnp.sum(kk * kk, axis=-1, keepdims=True) + L2_EPS)).reshape(B, T, C)
    k_t = k[None] * (1.0 + (a - 1.0) * k_a)

    def both(x):
        return jnp.stack([x, x[:, ::-1]])

    def orient(x2):
        return jnp.stack([x2[0], x2[1, :, ::-1]])

    def to_steps(x2):
        return jnp.moveaxis(x2.reshape(2, B, T, H, N), 2, 0)

    kk2 = both(kk)
    xs = (to_steps(both(r)), to_steps(orient(decay)), to_steps(orient(k_t)),
          to_steps(-kk2), to_steps(kk2 * orient(a)), to_steps(both(v)))
    S0 = jnp.zeros((2, B, H, N, N), jnp.float32)
    _, ys = lax.scan(rwkv7_step, S0, xs)
    ys = jnp.moveaxis(ys, 0, 2)
    y = ys[0] + ys[1, :, ::-1]
    mu = jnp.mean(y, axis=-1, keepdims=True)
    var = jnp.mean(jnp.square(y - mu), axis=-1, keepdims=True)
    yn = ((y - mu) * lax.rsqrt(var + GN_EPS)).reshape(B, T, C) * gn_w + gn_b
    coef = jnp.einsum('bthn,dbthn,hn->bth', r.reshape(B, T, H, N), k_t.reshape(2, B, T, H, N), r_k)
    bonus = (coef[..., None] * v.reshape(B, T, H, N)).reshape(B, T, C)
    return (yn + bonus) * g


def head_rms_norm(x, g):
    xf = x.astype(jnp.float32)
    y = xf * lax.rsqrt(jnp.mean(xf * xf, axis=-1, keepdims=True) + NORM_EPS)
    return (y * g.astype(jnp.float32)).astype(x.dtype)


def encoder_layer(x, ln1, w_in, q_gain, k_gain, rpb, shift_mu, w0, w_up, a0, a_up, g_up,
                  k_k, k_a, r_k, gn_w, gn_b, w_br_na, w_br_rw, w_out, ln2,
                  w_ffn_up, conv_w, conv_b, w_ffn_down):
    B, T, _ = x.shape
    h = rms_norm(x, ln1)
    p = h @ w_in
    q = head_rms_norm(p[..., :NA_WIDTH].reshape(B, T, NA_HEADS, NA_HEAD_DIM), q_gain)
    k = head_rms_norm(p[..., NA_WIDTH:2 * NA_WIDTH].reshape(B, T, NA_HEADS, NA_HEAD_DIM), k_gain)
    v = p[..., 2 * NA_WIDTH:3 * NA_WIDTH].reshape(B, T, NA_HEADS, NA_HEAD_DIM)
    o_na = neighborhood_attention(q, k, v, rpb)
    s0 = 3 * NA_WIDTH
    z = p[..., s0:s0 + RW_IN]
    z = z + shift_mu[0] * (shift_prev(z) - z) + shift_mu[1] * (shift_next(z) - z)
    o_rw = rwkv7_bidirectional(z, w0, w_up, a0, a_up, g_up, k_k, k_a, r_k, gn_w, gn_b).astype(x.dtype)
    s1 = s0 + RW_IN
    gate_na = jax.nn.sigmoid(p[..., s1:s1 + D_MODEL])
    gate_rw = jax.nn.sigmoid(p[..., s1 + D_MODEL:])
    merged = gate_na * (o_na @ w_br_na) + gate_rw * (o_rw @ w_br_rw)
    x = x + (merged @ w_out).astype(x.dtype)
    h = rms_norm(x, ln2)
    u = h @ w_ffn_up
    u = conv_w[0] * shift_prev(u) + conv_w[1] * u + conv_w[2] * shift_next(u) + conv_b
    val, gate = u[..., :FFN_DIM], u[..., FFN_DIM:]
    return x + ((jax.nn.silu(gate) * val) @ w_ffn_down).astype(x.dtype)


def setup_inputs(seed: int = 0) -> dict:
    key = jax.random.key(seed)
    ks = jax.random.split(key, 26)
    L, D, C = DEPTH, D_MODEL, RW_WIDTH

    def nrm(k, shape, scale):
        return jax.random.normal(k, shape, jnp.float32) * scale

    return {
        'x_prompt': nrm(ks[0], (BATCH, SEQ, D), 1.0),
        'x_sample': nrm(ks[1], (DEC_BATCH, DEC_SEQ, D), 1.0),
        'ln1': 1.0 + nrm(ks[2], (L, D), 0.05),
        'w_in': nrm(ks[3], (L, D, IN_WIDTH), D ** -0.5),
        'q_gain': 1.0 + nrm(ks[4], (L, NA_HEAD_DIM), 0.05),
        'k_gain': 1.0 + nrm(ks[5], (L, NA_HEAD_DIM), 0.05),
        'rpb': nrm(ks[6], (L, NA_HEADS, 2 * WIN_H - 1, 2 * WIN_W - 1), 0.1),
        'shift_mu': jax.random.uniform(ks[7], (L, 2, RW_IN), jnp.float32, 0.0, 0.5),
        'w0': jax.random.uniform(ks[8], (L, 2, C), jnp.float32, -6.0, 1.0),
        'w_up': nrm(ks[9], (L, 2, DECAY_LORA, C), 0.1 * DECAY_LORA ** -0.5),
        'a0': nrm(ks[10], (L, 2, C), 0.5),
        'a_up': nrm(ks[11], (L, 2, ICLR_LORA, C), 0.1 * ICLR_LORA ** -0.5),
        'g_up': nrm(ks[12], (L, GATE_LORA, C), GATE_LORA ** -0.5),
        'k_k': 0.85 + nrm(ks[13], (L, C), 0.05),
        'k_a': 1.0 + nrm(ks[14], (L, C), 0.05),
        'r_k': nrm(ks[15], (L, RW_HEADS, RW_HEAD), 0.1),
        'gn_w': 1.0 + nrm(ks[16], (L, C), 0.05),
        'gn_b': nrm(ks[17], (L, C), 0.02),
        'w_br_na': nrm(ks[18], (L, NA_WIDTH, D), NA_WIDTH ** -0.5),
        'w_br_rw': nrm(ks[19], (L, C, D), C ** -0.5),
        'w_out': nrm(ks[20], (L, D, D), D ** -0.5),
        'ln2': 1.0 + nrm(ks[21], (L, D), 0.05),
        'w_ffn_up': nrm(ks[22], (L, D, 2 * FFN_DIM), D ** -0.5),
        'conv_w': nrm(ks[23], (L, 3, 2 * FFN_DIM), 3 ** -0.5),
        'conv_b': nrm(ks[24], (L, 2 * FFN_DIM), 0.02),
        'w_ffn_down': nrm(ks[25], (L, FFN_DIM, D), FFN_DIM ** -0.5),
    }


def reference(x_prompt, x_sample, ln1, w_in, q_gain, k_gain, rpb, shift_mu, w0, w_up, a0, a_up,
              g_up, k_k, k_a, r_k, gn_w, gn_b, w_br_na, w_br_rw, w_out, ln2,
              w_ffn_up, conv_w, conv_b, w_ffn_down):
    def run(x):
        for l in range(DEPTH):
            x = encoder_layer(x, ln1[l], w_in[l], q_gain[l], k_gain[l], rpb[l], shift_mu[l],
                              w0[l], w_up[l], a0[l], a_up[l], g_up[l], k_k[l], k_a[l], r_k[l],
                              gn_w[l], gn_b[l], w_br_na[l], w_br_rw[l], w_out[l], ln2[l],
                              w_ffn_up[l], conv_w[l], conv_b[l], w_ffn_down[l])
        return x

    y_prompt = run(x_prompt)
    y_sample = run(x_sample)
    return (y_prompt, y_sample)
```

```python
from contextlib import ExitStack
import numpy as np
import ml_dtypes
import concourse.bass as bass
import concourse.mybir as mybir
from concourse.bass_utils import run_bass_kernel_spmd

F32 = mybir.dt.float32
BF16 = mybir.dt.bfloat16
AF = mybir.ActivationFunctionType
ALU = mybir.AluOpType
AX = mybir.AxisListType

D = 2048
NAW = 1024
RWC = 1024
RW_IN = 3488
INW = 10656
FFN = 5632
NEG = -30000.0
CDEC = -0.6065306597126334
NZB = 28
USE_ACT_NT = False

PV = {}
_o = 0
for _n, _c in [("ln1", 16), ("ln2", 16), ("qg", 1), ("kg", 1), ("mu0", NZB), ("mu1", NZB), ("w0", 16), ("a0", 16),
               ("kk", 8), ("ka", 8), ("rk", 8), ("gnw", 8), ("gnb", 8), ("cw0", 88), ("cw1", 88), ("cw2", 88), ("cb", 88)]:
    PV[_n] = _o
    _o += _c
NPV = _o
DV = {}
for _n, _c in [("qgs", 1), ("c0", NZB), ("m0n", NZB), ("m1n", NZB), ("omka", 8), ("cw0n", 88), ("cw2n", 88)]:
    DV[_n] = _o
    _o += _c
NPVT = _o

CT = {"ident": 0, "ones": 128, "blk2": 256, "mus": 384, "mui": 512, "mls": 640, "mli": 768, "seg": 896}
NCONST = 896 + 512


class Buf:
    __slots__ = ("w", "r")

    def __init__(self):
        self.w = {}
        self.r = {}


class Sched:
    def __init__(self, nc, st, ndma=40):
        self.nc = nc
        self.E = {"pe": nc.tensor, "dve": nc.vector, "act": nc.scalar, "pool": nc.gpsimd, "sp": nc.sync}
        self.semh = {}
        self.cnt = {}
        self.waited = {}
        for e in self.E:
            self.semh[e] = st.enter_context(nc.semaphore("s_" + e))
            self.cnt[e] = 0
            self.waited[e] = {}
        self.ndma = ndma
        self.dval = [0] * ndma
        for j in range(ndma):
            self.semh[("d", j)] = st.enter_context(nc.semaphore("d%d" % j))
        self.dpool = {"sp": list(range(0, ndma // 2)), "act": list(range(0, ndma // 2)), "pool": list(range(ndma // 2, ndma))}
        self.dma_i = {"sp": 0, "act": 0, "pool": 0}
        self.nins = 0

    def _deps(self, reads, writes):
        d = {}
        for b in reads:
            for k, v in b.w.items():
                if d.get(k, 0) < v:
                    d[k] = v
        for b in writes:
            for k, v in b.w.items():
                if d.get(k, 0) < v:
                    d[k] = v
            for k, v in b.r.items():
                if d.get(k, 0) < v:
                    d[k] = v
        return d

    def _wait(self, eng, d):
        w = self.waited[eng]
        for k, v in d.items():
            if w.get(k, 0) < v:
                self.E[eng].wait_ge(self.semh[k], v)
                w[k] = v
                self.nins += 1

    def op(self, eng, fn, reads=(), writes=(), inc=True):
        d = self._deps(reads, writes)
        if eng == "pe":
            d.pop("pe", None)
        self._wait(eng, d)
        ins = fn(self.E[eng])
        self.nins += 1
        if inc:
            self.cnt[eng] += 1
            ins.then_inc(self.semh[eng], 1)
            v = self.cnt[eng]
        else:
            v = self.cnt[eng] + 1
        for b in reads:
            b.r[eng] = v
        for b in writes:
            b.w = {eng: v}
            b.r = {}
        return ins

    def dma(self, q, out, in_, reads=(), writes=(), merge=False):
        d = self._deps(reads, writes)
        qk = "sp" if q in ("sp", "act") else "pool"
        pl = self.dpool[qk]
        j = pl[self.dma_i[qk] % len(pl)]
        self.dma_i[qk] += 1
        key = ("d", j)
        if d.get(key, 0) < self.dval[j]:
            d[key] = self.dval[j]
        self._wait(q, d)
        self.dval[j] += 16
        self.E[q].dma_start(out=out, in_=in_).then_inc(self.semh[key], 16)
        self.nins += 1
        v = self.dval[j]
        for b in reads:
            b.r[key] = v
        for b in writes:
            if merge:
                b.w[key] = v
            else:
                b.w = {key: v}
                b.r = {}

    def barrier(self):
        tgt = {e: self.cnt[e] for e in self.E}
        for j in range(self.ndma):
            tgt[("d", j)] = self.dval[j]
        for e in self.E:
            d = {k: v for k, v in tgt.items() if k != e and v > 0}
            self._wait(e, d)


def tile_ranges(T):
    NP = T // 128
    R = T // 64
    res = []
    for p in range(NP):
        need = set()
        for grids in ([(0, R)], [(0, R // 2), (R // 2, R // 2)]):
            for qr in range(2):
                r = 2 * p + qr
                for g0, rows in grids:
                    if g0 <= r < g0 + rows:
                        kh = min(8, rows)
                        rs = int(np.clip(r - g0 - kh // 2, 0, rows - kh)) + g0
                        for kr in range(rs, rs + kh):
                            need.add(kr // 2)
        lo, hi = min(need), max(need)
        res.append((lo, hi - lo + 1))
    return res


def na_slots(T):
    NP = T // 128
    HP = NP // 2
    rng = tile_ranges(T)
    special = sorted(set([0, 1, HP - 2, HP - 1, HP, HP + 1, NP - 2, NP - 1]) & set(range(NP)))
    interior = [p for p in range(NP) if p not in special]
    slot_of = {}
    for i, p in enumerate(special):
        slot_of[p] = i
    nslots = len(special)
    if interior:
        for p in interior:
            assert rng[p] == (p - 2, 5), (p, rng[p])
            slot_of[p] = nslots
        nslots += 1
    ntmax = max(n for _, n in rng)
    return rng, slot_of, nslots, ntmax, special, interior


def build_na_tables(rpb, T, sample_type):
    rng, slot_of, nslots, ntmax, special, interior = na_slots(T)
    R = T // 64
    grids = [(0, R // 2), (R // 2, R // 2)] if sample_type else [(0, R)]
    H = rpb.shape[0]
    tab = np.full((nslots, H, 128, ntmax, 128), NEG, np.float32)
    kc = np.arange(64)
    qc = np.arange(64)
    cs = np.clip(qc - 8, 0, 48)
    colv = (kc[:, None] >= cs[None, :]) & (kc[:, None] < cs[None, :] + 16)
    dxi = np.clip(kc[:, None] - qc[None, :] + 15, 0, 30)
    done = set()
    for p in range(T // 128):
        s = slot_of[p]
        if s in done:
            continue
        done.add(s)
        lo, nt = rng[p]
        for j in range(nt):
            for kr in range(2):
                for qr in range(2):
                    r = 2 * p + qr
                    krow = 2 * (lo + j) + kr
                    ok = False
                    for g0, rows in grids:
                        if g0 <= r < g0 + rows:
                            kh = min(8, rows)
                            rs = int(np.clip(r - g0 - kh // 2, 0, rows - kh)) + g0
                            ok = rs <= krow < rs + kh
                    if not ok:
                        continue
                    dy = krow - r + 7
                    vals = rpb[:, dy][:, dxi]
                    blk = np.where(colv[None], vals, np.float32(NEG))
                    tab[s, :, kr * 64:(kr + 1) * 64, j, qr * 64:(qr + 1) * 64] = blk
    return tab


def make_consts():
    c = np.zeros((128, NCONST), np.float32)
    i = np.arange(128)
    c[:, CT["ident"]:CT["ident"] + 128] = np.eye(128)
    c[:, CT["ones"]:CT["ones"] + 128] = 1.0
    c[:, CT["blk2"]:CT["blk2"] + 128] = (i[:, None] // 64 == i[None, :] // 64)
    c[:, CT["mus"]:CT["mus"] + 128] = (i[None, :] > i[:, None])
    c[:, CT["mui"]:CT["mui"] + 128] = (i[None, :] >= i[:, None])
    c[:, CT["mls"]:CT["mls"] + 128] = (i[None, :] < i[:, None])
    c[:, CT["mli"]:CT["mli"] + 128] = (i[None, :] <= i[:, None])
    seg = np.ones(512, np.float32)
    seg[::128] = 0.0
    c[:, CT["seg"]:CT["seg"] + 512] = seg[None, :]
    return c


def make_pvec(inp):
    def cols(v, n=None):
        v = np.asarray(v, np.float32).reshape(-1)
        nb = (len(v) + 127) // 128
        pad = np.zeros(nb * 128, np.float32)
        pad[:len(v)] = v
        return pad.reshape(nb, 128).T

    parts = [cols(inp["ln1"][0]), cols(inp["ln2"][0]), cols(inp["q_gain"][0]), cols(inp["k_gain"][0]),
             cols(inp["shift_mu"][0, 0]), cols(inp["shift_mu"][0, 1]),
             cols(inp["w0"][0]), cols(inp["a0"][0]), cols(inp["k_k"][0]), cols(inp["k_a"][0]), cols(inp["r_k"][0]),
             cols(inp["gn_w"][0]), cols(inp["gn_b"][0]),
             cols(inp["conv_w"][0, 0]), cols(inp["conv_w"][0, 1]), cols(inp["conv_w"][0, 2]), cols(inp["conv_b"][0])]
    pv = np.concatenate(parts, axis=1)
    assert pv.shape == (128, NPV), pv.shape
    return np.ascontiguousarray(pv)


def run_rr(gens):
    gens = list(gens)
    while gens:
        for gq in list(gens):
            try:
                next(gq)
            except StopIteration:
                gens.remove(gq)


class Ring:
    def __init__(self, items):
        self.items = [(t, Buf()) for t in items]
        self.i = 0

    def next(self):
        it = self.items[self.i % len(self.items)]
        self.i += 1
        return it


class KB:
    def __init__(self, T, debug=False):
        self.T = T
        self.NT = T // 128
        self.NG = T // 512
        self.debug = debug
        self.nc = nc = bass.Bass("TRN2", target_bir_lowering=False)
        self.dbg_outs = []
        di = lambda n, s, dt=F32: nc.dram_tensor(n, list(s), dt, kind="ExternalInput").ap()
        self.x = di("x", [T, D])
        self.flags = di("flags", [128, 4])
        self.pvec = di("pvec", [128, NPV])
        self.consts = di("consts", [128, NCONST])
        rng, slot_of, nslots, ntmax, special, interior = na_slots(T)
        self.na = (rng, slot_of, nslots, ntmax)
        self.natab = di("natab", [nslots, 8, 128, ntmax, 128])
        self.w_in = di("w_in", [D, INW])
        self.w_up = di("w_up", [128, RWC])
        self.a_up = di("a_up", [128, RWC])
        self.g_up = di("g_up", [160, RWC])
        self.w_br_na = di("w_br_na", [NAW, D])
        self.w_br_rw = di("w_br_rw", [RWC, D])
        self.w_out = di("w_out", [D, D])
        self.w_ffn_up = di("w_ffn_up", [D, 2 * FFN])
        self.w_ffn_down = di("w_ffn_down", [FFN, D])
        self.y = nc.dram_tensor("y", [T, D], F32, kind="ExternalOutput").ap()

    def load_cf(self, st, name):
        cf = st.enter_context(self.nc.sbuf_tensor(name, [128, NCONST], F32))
        B = Buf()
        self.S.dma("sp", cf[:], self.consts[:, :], writes=[B])
        return cf, B

    def scratch(self, name, shape, dt):
        kind = "ExternalOutput" if self.debug else "Internal"
        if self.debug:
            self.dbg_outs.append(name)
        return self.nc.dram_tensor(name, list(shape), dt, kind=kind).ap()

    def build(self, upto="E2"):
        nc = self.nc
        T = self.T
        with ExitStack() as st:
            self.st = st
            self.S = S = Sched(nc, st)
            sb = lambda n, s, dt: st.enter_context(nc.sbuf_tensor(n, list(s), dt))
            self.pv = sb("pv", [128, NPVT], F32)
            self.cb = sb("cbf", [128, NCONST], BF16)
            self.fl = sb("fl", [128, 4], F32)
            st_cf = ExitStack()
            self.cf = st_cf.enter_context(nc.sbuf_tensor("cf", [128, NCONST], F32))
            B0 = Buf()
            S.dma("sp", self.pv[:, 0:NPV], self.pvec[:, :], writes=[B0])
            S.dma("sp", self.cf[:], self.consts[:, :], writes=[B0])
            S.dma("sp", self.fl[:], self.flags[:, :], writes=[B0])
            S.barrier()
            pv = self.pv
            V = lambda e: e
            S.op("dve", lambda e: e.tensor_copy(out=self.cb[:], in_=self.cf[:]), reads=[B0], writes=[B0])
            S.op("dve", lambda e: e.tensor_scalar(pv[:, DV["qgs"]:DV["qgs"] + 1], pv[:, PV["qg"]:PV["qg"] + 1], 128.0 ** -0.5, None, ALU.mult), writes=[B0])
            c0 = pv[:, DV["c0"]:DV["c0"] + NZB]
            S.op("dve", lambda e: e.tensor_tensor(out=c0, in0=pv[:, PV["mu0"]:PV["mu0"] + NZB], in1=pv[:, PV["mu1"]:PV["mu1"] + NZB], op=ALU.add), writes=[B0])
            S.op("dve", lambda e: e.tensor_scalar(c0, c0, -1.0, 1.0, ALU.mult, ALU.add), writes=[B0])
            fm1 = self.fl[:, 2:3]
            for a, b, n in [("m0n", "mu0", NZB), ("m1n", "mu1", NZB), ("cw0n", "cw0", 88), ("cw2n", "cw2", 88)]:
                S.op("dve", lambda e, a=a, b=b, n=n: e.tensor_scalar(pv[:, DV[a]:DV[a] + n], pv[:, PV[b]:PV[b] + n], fm1, None, ALU.mult), writes=[B0])
            S.op("dve", lambda e: e.tensor_scalar(pv[:, DV["omka"]:DV["omka"] + 8], pv[:, PV["ka"]:PV["ka"] + 8], -1.0, 1.0, ALU.mult, ALU.add), writes=[B0])
            S.barrier()
            st_cf.close()
            self.cf = None
            self.identb = self.cb[:, CT["ident"]:CT["ident"] + 128]
            self.onesb = self.cb[:, CT["ones"]:CT["ones"] + 128]
            self.blk2b = self.cb[:, CT["blk2"]:CT["blk2"] + 128]
            self.qT = self.scratch("qT_s", [8, 128, T], BF16)
            self.kT = self.scratch("kT_s", [8, 128, T], BF16)
            self.Vna = self.scratch("Vna_s", [T, NAW], BF16)
            self.zT = self.scratch("zT_s", [RW_IN, T], F32)
            self.gT = self.scratch("gT_s", [2 * D, T], F32)
            self.phaseA()
            S.barrier()
            order = ["A", "B", "C1", "C2", "C3", "D1", "D2", "E1", "E2"]
            for ph in order[1:order.index(upto) + 1]:
                getattr(self, "phase" + ph)()
                S.barrier()
            return self.finish()

    def finish(self):
        self.S.barrier()
        return self.nc

    def norm_T(self, src, srcB, gcol0, dst_fn, R):
        S = self.S
        pv = self.pv
        junk, jB = R["junk"].next()
        ss, sB = R["ss"].next()
        S.op("dve", lambda e: e.memset(ss[:], 0.0), writes=[sB])
        yield
        S.op("act", lambda e: e.activation(out=junk[:], in_=src, func=AF.Square, accum_out=ss[:, 0:1]), reads=[srcB], writes=[jB, sB])
        yield
        S.op("dve", lambda e: e.tensor_scalar(ss[:, 1:2], ss[:, 0:1], 1.0 / D, 1e-6, ALU.mult, ALU.add), reads=[sB], writes=[sB])
        yield
        S.op("act", lambda e: e.sqrt(ss[:, 3:4], ss[:, 1:2]), reads=[sB], writes=[sB])
        yield
        S.op("dve", lambda e: e.reciprocal(ss[:, 2:3], ss[:, 3:4]), reads=[sB], writes=[sB])
        yield
        xs, xB = R["xs"].next()
        S.op("act", lambda e: e.activation(out=xs[:], in_=src, func=AF.Copy, scale=ss[:, 2:3]), reads=[srcB, sB], writes=[xB])
        yield
        for hlf in range(2):
            pT, pB = R["psT"].next()
            for c8 in range(8):
                c = hlf * 8 + c8
                S.op("pe", lambda e, c=c, c8=c8: e.transpose(pT[:, c8 * 128:(c8 + 1) * 128], xs[:, c * 128:(c + 1) * 128], self.identb),
                     reads=[xB], writes=[pB], inc=(c8 == 7))
            yield
            for c8 in range(8):
                c = hlf * 8 + c8
                dst, dB = dst_fn(c)
                g = pv[:, gcol0 + c:gcol0 + c + 1]
                if c8 % 2 == 0:
                    S.op("act", lambda e, dst=dst, c8=c8, g=g: e.activation(out=dst, in_=pT[:, c8 * 128:(c8 + 1) * 128], func=AF.Copy, scale=g), reads=[pB], writes=[dB])
                else:
                    S.op("dve", lambda e, dst=dst, c8=c8, g=g: e.tensor_scalar(dst, pT[:, c8 * 128:(c8 + 1) * 128], g, None, ALU.mult), reads=[pB], writes=[dB])
                    yield

    def phaseA(self):
        nc, S, T, NT, NG = self.nc, self.S, self.T, self.NT, self.NG
        pv = self.pv
        with ExitStack() as st:
            sb = lambda n, s, dt: st.enter_context(nc.sbuf_tensor(n, list(s), dt))
            ps = lambda n, s, dt: st.enter_context(nc.psum_tensor(n, list(s), dt))
            hT = sb("hT", [128, 16, T], BF16)
            hB = [Buf() for _ in range(NT)]
            psTr = Ring([ps("psT%d" % i, [128, 1024], BF16) for i in range(2)])
            with ExitStack() as st0:
                sb0 = lambda n, s, dt: st0.enter_context(nc.sbuf_tensor(n, list(s), dt))
                R = {"junk": Ring([sb0("junk", [128, D], BF16)]),
                     "ss": Ring([sb0("ss%d" % i, [128, 4], F32) for i in range(2)]),
                     "xs": Ring([sb0("xs%d" % i, [128, D], BF16) for i in range(2)]),
                     "psT": psTr}
                xt = Ring([sb0("xt%d" % i, [128, D], F32) for i in range(2)])
                def a0body(i):
                    x_t, xB = xt.next()
                    S.dma("sp", x_t[:], self.x[i * 128:(i + 1) * 128, :], writes=[xB])
                    yield
                    yield from self.norm_T(x_t[:], xB, PV["ln1"], lambda c, i=i: (hT[:, c, i * 128:(i + 1) * 128], hB[i]), R)
                for i0 in range(0, NT, 2):
                    run_rr([a0body(i0 + j) for j in range(2) if i0 + j < NT])
                S.barrier()
            blocks = []
            for h in range(8):
                blocks.append(("q", h * 128, 128, h))
            for h in range(8):
                blocks.append(("k", NAW + h * 128, 128, h))
            for h in range(8):
                blocks.append(("v", 2 * NAW + h * 128, 128, h))
            for zb in range(NZB):
                blocks.append(("z", 3 * NAW + zb * 128, min(128, RW_IN - zb * 128), zb))
            for g in range(32):
                blocks.append(("g", 3 * NAW + RW_IN + g * 128, 128, g))
            self.R_A = R
            self.ws_proj(st, blocks, self.w_in, hT, hB, 16, self.postA)

    def ws_proj(self, st, blocks, W, actT, actB, KT, post):
        nc, S, T, NG = self.nc, self.S, self.T, self.NG
        self._nws = getattr(self, "_nws", 0) + 1
        tag = "u%d_" % self._nws
        sb = lambda n, s, dt: st.enter_context(nc.sbuf_tensor(tag + n, list(s), dt))
        ps = lambda n, s, dt: st.enter_context(nc.psum_tensor(tag + n, list(s), dt))
        wst = sb("wst", [128, KT, 128], F32)
        wstB = Buf()
        wbf = Ring([sb("wbf%d" % i, [128, KT, 128], BF16) for i in range(2)])
        pm = Ring([ps("pm%d" % i, [128, 512], F32) for i in range(4)])
        Wv = W.rearrange("(k p) n -> p k n", p=128)
        ctx = {"st": st, "sb": sb, "ps": ps}

        def load(b):
            kind, c0, n, idx = blocks[b]
            S.dma("sp", wst[:, :, 0:n], Wv[:, :, c0:c0 + n], writes=[wstB])

        def cast(b):
            kind, c0, n, idx = blocks[b]
            w, wB = wbf.next()
            h = KT // 2
            S.op("act", lambda e: e.activation(out=w[:, 0:h, 0:n], in_=wst[:, 0:h, 0:n], func=AF.Copy), reads=[wstB], writes=[wB])
            S.op("dve", lambda e: e.tensor_copy(out=w[:, h:KT, 0:n], in_=wst[:, h:KT, 0:n]), reads=[wstB, wB], writes=[wB])
            return w, wB

        load(0)
        cur = cast(0)
        if len(blocks) > 1:
            load(1)
        for b in range(len(blocks)):
            kind, c0, n, idx = blocks[b]
            w, wB = cur
            nxt = None
            for g in range(NG):
                p, pB = pm.next()
                for k in range(KT):
                    S.op("pe", lambda e, k=k: e.matmul(p[0:n, :], w[:, k, 0:n], actT[:, k, g * 512:(g + 1) * 512], start=(k == 0), stop=(k == KT - 1)),
                         reads=[wB] + actB[g * 4:(g + 1) * 4], writes=[pB], inc=(k == KT - 1))
                post(ctx, blocks[b], g, p, pB)
                if g == min(1, NG - 1) and b + 1 < len(blocks):
                    nxt = cast(b + 1)
                    if b + 2 < len(blocks):
                        load(b + 2)
            post(ctx, blocks[b], NG, None, None)
            cur = nxt

    def postA(self, ctx, blk, g, p, pB):
        nc, S, T, NG = self.nc, self.S, self.T, self.NG
        pv = self.pv
        kind, c0, n, idx = blk
        if "A" not in ctx:
            sb, ps = ctx["sb"], ctx["ps"]
            ctx["A"] = {
                "sq": Ring([sb("sq%d" % i, [128, 512], BF16) for i in range(2)]),
                "pss": Ring([ps("pss%d" % i, [128, 512], F32) for i in range(2)]),
                "rstd": Ring([sb("rstd%d" % i, [128, 512], F32) for i in range(2)]),
                "ob": Ring([sb("ob%d" % i, [128, 512], BF16) for i in range(3)]),
                "vt": Ring([sb("vt%d" % i, [128, 4, 128], BF16) for i in range(2)]),
                "zb": Ring([sb("zb%d" % i, [128, T + 2], F32) for i in range(2)]),
                "of": Ring([sb("of%d" % i, [128, 512], F32) for i in range(2)]),
            }
            for z, zB in ctx["A"]["zb"].items:
                S.op("pool", lambda e, z=z: e.memset(z[:, 0:1], 0.0), writes=[zB])
                S.op("pool", lambda e, z=z: e.memset(z[:, T + 1:T + 2], 0.0), writes=[zB])
        A = ctx["A"]
        gs = slice(g * 512, (g + 1) * 512)
        if kind in ("q", "k"):
            if p is None:
                return
            sq, sqB = A["sq"].next()
            S.op("act", lambda e: e.activation(out=sq[:], in_=p[:], func=AF.Square), reads=[pB], writes=[sqB])
            pss, pssB = A["pss"].next()
            S.op("pe", lambda e: e.matmul(pss[:], self.onesb, sq[:], start=True, stop=True), reads=[sqB], writes=[pssB])
            rs, rsB = A["rstd"].next()
            S.op("dve", lambda e: e.tensor_scalar(rs[:], pss[:], 1.0 / 128, 1e-6, ALU.mult, ALU.add), reads=[pssB], writes=[rsB])
            S.op("act", lambda e: e.sqrt(rs[:], rs[:]), reads=[rsB], writes=[rsB])
            S.op("dve", lambda e: e.reciprocal(rs[:], rs[:]), reads=[rsB], writes=[rsB])
            ob, obB = A["ob"].next()
            gc = pv[:, DV["qgs"]:DV["qgs"] + 1] if kind == "q" else pv[:, PV["kg"]:PV["kg"] + 1]
            S.op("dve", lambda e: e.scalar_tensor_tensor(out=ob[:], in0=p[:], scalar=gc, in1=rs[:], op0=ALU.mult, op1=ALU.mult), reads=[pB, rsB], writes=[obB])
            dst = (self.qT if kind == "q" else self.kT)[idx, :, gs]
            S.dma("pool", dst, ob[:], reads=[obB])
        elif kind == "v":
            if p is None:
                return
            ob, obB = A["ob"].next()
            S.op("act", lambda e: e.activation(out=ob[:], in_=p[:], func=AF.Copy), reads=[pB], writes=[obB])
            pV, pVB = self.R_A["psT"].next()
            for j in range(4):
                S.op("pe", lambda e, j=j: e.transpose(pV[:, j * 128:(j + 1) * 128], ob[:, j * 128:(j + 1) * 128], self.identb), reads=[obB], writes=[pVB], inc=(j == 3))
            vt, vtB = A["vt"].next()
            S.op("dve", lambda e: e.tensor_copy(out=vt[:].rearrange("p a b -> p (a b)"), in_=pV[:, 0:512]), reads=[pVB], writes=[vtB])
            dst = self.Vna.rearrange("(n p) c -> p n c", p=128)[:, g * 4:(g + 1) * 4, idx * 128:(idx + 1) * 128]
            S.dma("pool", dst, vt[:], reads=[vtB])
        elif kind == "g":
            if p is None:
                return
            ob, obB = A["of"].next()
            S.op("act", lambda e: e.activation(out=ob[:], in_=p[:], func=AF.Sigmoid), reads=[pB], writes=[obB])
            S.dma("pool", self.gT[idx * 128:(idx + 1) * 128, gs], ob[:], reads=[obB])
        elif kind == "z":
            if g == 0:
                ctx["zcur"] = A["zb"].next()
                ctx["zgB"] = ctx.setdefault(("zgB", id(ctx["zcur"][1])), [Buf() for _ in range(NG)])
            z, zB0 = ctx["zcur"]
            zgB = ctx["zgB"]
            if p is not None:
                S.op("act", lambda e: e.activation(out=z[0:n, 1 + g * 512:1 + (g + 1) * 512], in_=p[0:n, :], func=AF.Copy), reads=[pB, zB0], writes=[zgB[g]])
            gm = g - 1
            if gm < 0:
                return
            self.shift_mix(z, zgB, zB0, gm, n, pv[:, DV["c0"] + idx:DV["c0"] + idx + 1], pv[:, PV["mu0"] + idx:PV["mu0"] + idx + 1],
                           pv[:, PV["mu1"] + idx:PV["mu1"] + idx + 1], pv[:, DV["m0n"] + idx:DV["m0n"] + idx + 1],
                           pv[:, DV["m1n"] + idx:DV["m1n"] + idx + 1], None, A["of"],
                           lambda o, oB: S.dma("pool", self.zT[idx * 128:idx * 128 + n, gm * 512:(gm + 1) * 512], o[0:n, :], reads=[oB]))

    def shift_mix(self, z, zgB, zB0, gm, n, c_c, c_p, c_n, c_pfix, c_nfix, bias, ring, sink):
        S, T, NG = self.S, self.T, self.NG
        rd = [zgB[i] for i in (gm - 1, gm, gm + 1) if 0 <= i < NG] + [zB0]
        o, oB = ring.next() if not isinstance(ring, tuple) else ring
        a = 1 + gm * 512
        if bias is None:
            S.op("act", lambda e: e.activation(out=o[0:n, :], in_=z[0:n, a:a + 512], func=AF.Copy, scale=c_c[0:n]), reads=rd, writes=[oB])
        else:
            S.op("act", lambda e: e.activation(out=o[0:n, :], in_=z[0:n, a:a + 512], func=AF.Identity, scale=c_c[0:n], bias=bias[0:n]), reads=rd, writes=[oB])
        S.op("dve", lambda e: e.scalar_tensor_tensor(out=o[0:n, :], in0=z[0:n, a - 1:a + 511], scalar=c_p[0:n], in1=o[0:n, :], op0=ALU.mult, op1=ALU.add), reads=rd + [oB], writes=[oB])
        S.op("dve", lambda e: e.scalar_tensor_tensor(out=o[0:n, :], in0=z[0:n, a + 1:a + 513], scalar=c_n[0:n], in1=o[0:n, :], op0=ALU.mult, op1=ALU.add), reads=rd + [oB], writes=[oB])
        hb = T // 2
        if gm * 512 == hb:
            S.op("dve", lambda e: e.scalar_tensor_tensor(out=o[0:n, 0:1], in0=z[0:n, hb:hb + 1], scalar=c_pfix[0:n], in1=o[0:n, 0:1], op0=ALU.mult, op1=ALU.add), reads=rd + [oB], writes=[oB])
        if (gm + 1) * 512 == hb:
            S.op("dve", lambda e: e.scalar_tensor_tensor(out=o[0:n, 511:512], in0=z[0:n, hb + 1:hb + 2], scalar=c_nfix[0:n], in1=o[0:n, 511:512], op0=ALU.mult, op1=ALU.add), reads=rd + [oB], writes=[oB])
        sink(o, oB)


def make_in_maps(inp, xs, sample_types, T):
    pvec = make_pvec(inp)
    consts = make_consts()
    sq = lambda a: np.ascontiguousarray(np.asarray(a, np.float32)[0])
    shared = {"pvec": pvec, "consts": consts, "w_in": sq(inp["w_in"]),
              "w_up": sq(inp["w_up"]).reshape(128, RWC), "a_up": sq(inp["a_up"]).reshape(128, RWC), "g_up": sq(inp["g_up"]),
              "w_br_na": sq(inp["w_br_na"]), "w_br_rw": sq(inp["w_br_rw"]), "w_out": sq(inp["w_out"]),
              "w_ffn_up": sq(inp["w_ffn_up"]), "w_ffn_down": sq(inp["w_ffn_down"])}
    rpb = sq(inp["rpb"])
    tabs = {False: build_na_tables(rpb, T, False), True: build_na_tables(rpb, T, True)}
    maps = []
    for x, stype in zip(xs, sample_types):
        f = 0.0 if stype else 1.0
        flags = np.zeros((128, 4), np.float32)
        flags[:, 0] = f
        flags[:, 1] = 1.0 - f
        flags[:, 2] = f - 1.0
        m = dict(shared)
        m["x"] = np.ascontiguousarray(x, dtype=np.float32)
        m["flags"] = flags
        m["natab"] = tabs[stype]
        maps.append(m)
    return maps


def phaseB(self):
    nc, S, T, NT = self.nc, self.S, self.T, self.NT
    rng, slot_of, nslots, ntmax = self.na
    self.onaT = self.scratch("onaT_s", [NAW, T], BF16)
    with ExitStack() as st:
        sb = lambda n, s, dt: st.enter_context(nc.sbuf_tensor(n, list(s), dt))
        ps = lambda n, s, dt: st.enter_context(nc.psum_tensor(n, list(s), dt))
        qh = Ring([sb("qh%d" % i, [128, T], BF16) for i in range(2)])
        kh = Ring([sb("kh%d" % i, [128, T], BF16) for i in range(2)])
        vh = Ring([sb("vh%d" % i, [128, NT, 128], BF16) for i in range(2)])
        tb = Ring([sb("tb%d" % i, [128, nslots, ntmax * 128], F32) for i in range(2)])
        oh = Ring([sb("oh%d" % i, [128, T], BF16) for i in range(2)])
        psS = Ring([ps("psS%d" % i, [128, 1024], F32) for i in range(2)])
        psO = Ring([ps("psO%d" % i, [128, 512], F32) for i in range(3)])
        epre = Ring([sb("epre%d" % i, [128, ntmax * 128], F32) for i in range(3)])
        ebf = Ring([sb("ebf%d" % i, [128, ntmax * 128], BF16) for i in range(3)])
        rec = Ring([sb("rec%d" % i, [128, 128], F32) for i in range(3)])
        Vv = self.Vna.rearrange("(n p) c -> p n c", p=128)
        for h in range(8):
            q, qB = qh.next()
            k, kB = kh.next()
            v, vB = vh.next()
            tab, tB = tb.next()
            o, oB = oh.next()
            S.dma("sp", q[:], self.qT[h, :, :], writes=[qB])
            S.dma("sp", k[:], self.kT[h, :, :], writes=[kB])
            S.dma("sp", v[:], Vv[:, :, h * 128:(h + 1) * 128], writes=[vB])
            S.dma("sp", tab[:], self.natab[:, h].rearrange("s p j q -> p s (j q)"), writes=[tB])
            def pbody(p):
                lo, nt = rng[p]
                s = slot_of[p]
                pS, pSB = psS.next()
                for j in range(nt):
                    S.op("pe", lambda e, j=j: e.matmul(pS[:, j * 128:(j + 1) * 128], k[:, (lo + j) * 128:(lo + j + 1) * 128], q[:, p * 128:(p + 1) * 128], start=True, stop=True),
                         reads=[kB, qB], writes=[pSB], inc=(j == nt - 1))
                ep, epB = epre.next()
                w = nt * 128
                for a, b in ([(0, min(w, 512))] + ([(512, w)] if w > 512 else [])):
                    S.op("dve", lambda e, a=a, b=b: e.tensor_tensor(out=ep[:, a:b], in0=pS[:, a:b], in1=tab[:, s, a:b], op=ALU.add), reads=[pSB, tB], writes=[epB])
                    yield
                eb, ebB = ebf.next()
                S.op("act", lambda e: e.activation(out=eb[:, 0:w], in_=ep[:, 0:w], func=AF.Exp), reads=[epB], writes=[ebB])
                yield
                pO, pOB = psO.next()
                for j in range(nt):
                    S.op("pe", lambda e, j=j: e.matmul(pO[:, 0:128], v[:, lo + j, :], eb[:, j * 128:(j + 1) * 128], start=(j == 0), stop=(j == nt - 1)),
                         reads=[vB, ebB], writes=[pOB], inc=False)
                for j in range(nt):
                    S.op("pe", lambda e, j=j: e.matmul(pO[:, 128:256], self.onesb, eb[:, j * 128:(j + 1) * 128], start=(j == 0), stop=(j == nt - 1)),
                         reads=[ebB], writes=[pOB], inc=(j == nt - 1))
                rc, rcB = rec.next()
                S.op("dve", lambda e: e.reciprocal(rc[:], pO[:, 128:256]), reads=[pOB], writes=[rcB])
                yield
                S.op("dve", lambda e: e.tensor_tensor(out=o[:, p * 128:(p + 1) * 128], in0=pO[:, 0:128], in1=rc[:], op=ALU.mult), reads=[pOB, rcB], writes=[oB])
                yield
            for p0 in range(0, NT, 2):
                gens = [pbody(p0 + i) for i in range(2) if p0 + i < NT]
                while gens:
                    for gq in list(gens):
                        try:
                            next(gq)
                        except StopIteration:
                            gens.remove(gq)
            S.dma("pool", self.onaT[h * 128:(h + 1) * 128, :], o[:], reads=[oB])


KB.phaseB = phaseB


def phaseC(self):
    self.phaseC1()
    self.S.barrier()
    self.phaseC2()
    self.S.barrier()
    self.phaseC3()


def phaseC1(self):
    nc, S, T, NT, NG = self.nc, self.S, self.T, self.NT, self.NG
    pv = self.pv
    self.opT = self.scratch("opT_s", [2, 4, RWC, T], BF16)
    self.optm = self.scratch("optm_s", [2, 2, T, RWC], BF16)
    self.Vtm = self.scratch("Vtm_s", [T, RWC], BF16)
    self.ggT = self.scratch("ggT_s", [RWC, T], F32)
    self.cpT = self.scratch("cpT_s", [RWC, T], BF16)
    self.gam = self.st.enter_context(nc.sbuf_tensor("gam", [128, 2, 8, NT], F32))
    self.gamB = Buf()
    gam = self.gam
    with ExitStack() as st:
        sb = lambda n, s, dt: st.enter_context(nc.sbuf_tensor(n, list(s), dt))
        ps = lambda n, s, dt: st.enter_context(nc.psum_tensor(n, list(s), dt))
        wB = Buf()
        wl = {}
        stg = sb("c1stg", [128, RWC], F32)
        stgB = Buf()
        for nm, src, rows in [("wup", self.w_up[:, :], 128), ("aup", self.a_up[:, :], 128), ("gupa", self.g_up[0:128, :], 128), ("gupb", self.g_up[128:160, :], 32)]:
            wl[nm] = sb("c1" + nm, [128, RWC], BF16)
            S.dma("sp", stg[0:rows, :], src, writes=[stgB])
            S.op("dve", lambda e, nm=nm, rows=rows: e.tensor_copy(out=wl[nm][0:rows, :], in_=stg[0:rows, :]), reads=[stgB], writes=[wB])
        f32r = Ring([sb("c1f%d" % i, [128, 512], F32) for i in range(44)])
        b16r = Ring([sb("c1b%d" % i, [128, 512], BF16) for i in range(28)])
        b16l = Ring([sb("c1l%d" % i, [128, 512], BF16) for i in range(8)])
        psr = Ring([ps("c1p%d" % i, [128, 512], F32) for i in range(6)])
        psT = Ring([ps("c1t%d" % i, [128, 1024], BF16) for i in range(2)])
        cf1, cf1B = self.load_cf(st, "cf_c1")
        segm = cf1[:, CT["seg"]:CT["seg"] + 512]
        eng_rr = [0]

        deferred = []

        def defer(dst, src, reads=()):
            deferred.append((dst, src, list(reads)))

        def flush():
            for dst, src, rd in deferred:
                S.dma("sp", dst, src, reads=rd)
            del deferred[:]

        def ew(fn, reads, writes):
            eng_rr[0] += 1
            S.op("pool" if eng_rr[0] % 7 in (0, 3) else "dve", fn, reads=reads, writes=writes)

        def ldf(rows0, n, g):
            t, tB = f32r.next()
            S.dma("sp", t[0:n, :], self.zT[rows0:rows0 + n, g * 512:(g + 1) * 512], writes=[tB])
            return t, tB

        def to_tm(src, srcB, dst_ap):
            pT, pTB = psT.next()
            for j in range(4):
                S.op("pe", lambda e, j=j: e.transpose(pT[:, j * 128:(j + 1) * 128], src[:, j * 128:(j + 1) * 128], self.identb), reads=[srcB], writes=[pTB], inc=(j == 3))
            o, oB = b16r.next()
            S.op("act", lambda e: e.activation(out=o[:], in_=pT[:, 0:512], func=AF.Copy), reads=[pTB], writes=[oB])
            defer(dst_ap, o[:].rearrange("p (j c) -> p j c", c=128), reads=[oB])

        for g in range(NG):
            gs = slice(g * 512, (g + 1) * 512)
            dw, dwB = ldf(3072, 128, g)
            da, daB = ldf(3200, 128, g)
            dg1, dg1B = ldf(3328, 128, g)
            dg2, dg2B = ldf(3456, 32, g)
            tdw, tdwB = b16l.next()
            S.op("act", lambda e: e.activation(out=tdw[:], in_=dw[:], func=AF.Tanh), reads=[dwB], writes=[tdwB])
            dab, dabB = b16l.next()
            S.op("dve", lambda e: e.tensor_copy(out=dab[:], in_=da[:]), reads=[daB], writes=[dabB])
            sg1, sg1B = b16l.next()
            S.op("act", lambda e: e.activation(out=sg1[:], in_=dg1[:], func=AF.Sigmoid), reads=[dg1B], writes=[sg1B])
            sg2, sg2B = b16l.next()
            S.op("act", lambda e: e.activation(out=sg2[0:32, :], in_=dg2[0:32, :], func=AF.Sigmoid), reads=[dg2B], writes=[sg2B])
            import os
            LV = int(os.environ.get("C1LV", "9"))
            for cb in range(8):
                if LV < 2:
                    break
                cs_ = slice(cb * 128, (cb + 1) * 128)
                pcol = lambda nm, off=0: pv[:, PV[nm] + off + cb:PV[nm] + off + cb + 1]
                rT, rB = ldf(cb * 128, 128, g)
                kT, kB = ldf(1024 + cb * 128, 128, g)
                vT, vB = ldf(2048 + cb * 128, 128, g)
                flush()
                kkp, kkpB = f32r.next()
                S.op("act", lambda e: e.activation(out=kkp[:], in_=kT[:], func=AF.Copy, scale=pcol("kk")), reads=[kB], writes=[kkpB])
                sq, sqB = b16r.next()
                S.op("act", lambda e: e.activation(out=sq[:], in_=kkp[:], func=AF.Square), reads=[kkpB], writes=[sqB])
                pss, pssB = psr.next()
                S.op("pe", lambda e: e.matmul(pss[:], self.blk2b, sq[:], start=True, stop=True), reads=[sqB], writes=[pssB])
                rn, rnB = f32r.next()
                S.op("dve", lambda e: e.tensor_scalar(rn[:], pss[:], 1e-12, None, ALU.add), reads=[pssB], writes=[rnB])
                S.op("act", lambda e: e.sqrt(rn[:], rn[:]), reads=[rnB], writes=[rnB])
                S.op("dve", lambda e: e.reciprocal(rn[:], rn[:]), reads=[rnB], writes=[rnB])
                kk, kkB = f32r.next()
                ew(lambda e: e.tensor_tensor(out=kk[:], in0=kkp[:], in1=rn[:], op=ALU.mult), [kkpB, rnB], [kkB])
                if LV < 3:
                    continue
                vb, vbB = b16r.next()
                S.op("act", lambda e: e.activation(out=vb[:], in_=vT[:], func=AF.Copy), reads=[vB], writes=[vbB])
                to_tm(vb, vbB, self.Vtm.rearrange("(n p) c -> p n c", p=128)[:, g * 4:(g + 1) * 4, cs_])
                pg, pgB = psr.next()
                S.op("pe", lambda e: e.matmul(pg[:], wl["gupa"][:, cs_], sg1[:], start=True, stop=False), reads=[wB, sg1B], writes=[pgB], inc=False)
                S.op("pe", lambda e: e.matmul(pg[:], wl["gupb"][0:32, cs_], sg2[0:32, :], start=False, stop=True), reads=[wB, sg2B], writes=[pgB])
                gg, ggB = f32r.next()
                S.op("act", lambda e: e.activation(out=gg[:], in_=pg[:], func=AF.Copy), reads=[pgB], writes=[ggB])
                defer(self.ggT[cs_, gs], gg[:], reads=[ggB])
                kts = []
                if LV < 4:
                    continue
                def dbody(d):
                    ds_ = slice(d * 64, (d + 1) * 64)
                    pw, pwB = psr.next()
                    S.op("pe", lambda e: e.matmul(pw[:], wl["wup"][ds_, cs_], tdw[ds_, :], start=True, stop=True), reads=[wB, tdwB], writes=[pwB])
                    yield
                    sig, sigB = f32r.next()
                    S.op("act", lambda e: e.activation(out=sig[:], in_=pw[:], func=AF.Sigmoid, bias=pcol("w0", d * 8)), reads=[pwB], writes=[sigB])
                    yield
                    pa, paB = psr.next()
                    S.op("pe", lambda e: e.matmul(pa[:], wl["aup"][ds_, cs_], dab[ds_, :], start=True, stop=True), reads=[wB, dabB], writes=[paB])
                    yield
                    a, aB = f32r.next()
                    S.op("act", lambda e: e.activation(out=a[:], in_=pa[:], func=AF.Sigmoid, bias=pcol("a0", d * 8)), reads=[paB], writes=[aB])
                    yield
                    cs, csB = f32r.next()
                    S.op("dve", lambda e: e.tensor_tensor_scan(out=cs[:], data0=segm, data1=sig[:], initial=0.0, op0=ALU.mult, op1=ALU.add), reads=[sigB, cf1B], writes=[csB])
                    yield
                    x3n, x3nB = f32r.next()
                    for j in range(4):
                        js = slice(j * 128, (j + 1) * 128)
                        S.op("pool", lambda e, js=js, j=j: e.tensor_scalar(x3n[:, js], cs[:, js], cs[:, j * 128 + 127:j * 128 + 128], None, ALU.subtract), reads=[csB], writes=[x3nB])
                        yield
                    x2, x2B = f32r.next()
                    ew(lambda e: e.tensor_tensor(out=x2[:], in0=cs[:], in1=sig[:], op=ALU.subtract), [csB, sigB], [x2B])
                    yield
                    if d == 0:
                        inc_src, inc_B = cs, csB
                        e_specs = [(cs, csB, CDEC), (x2, x2B, CDEC), (cs, csB, -CDEC), (x3n, x3nB, -CDEC)]
                    else:
                        x4, x4B = f32r.next()
                        ew(lambda e: e.tensor_tensor(out=x4[:], in0=sig[:], in1=x3n[:], op=ALU.subtract), [sigB, x3nB], [x4B])
                        yield
                        e_specs = [(x4, x4B, CDEC), (x3n, x3nB, -CDEC), (x4, x4B, -CDEC), (x2, x2B, CDEC)]
                    ex = []
                    for src, srcB, sc in e_specs:
                        t, tB = f32r.next()
                        S.op("act", lambda e, t=t, src=src, sc=sc: e.activation(out=t[:], in_=src[:], func=AF.Exp, scale=sc), reads=[srcB], writes=[tB])
                        yield
                        ex.append((t, tB))
                    (ei, eiB), (ee, eeB), (ev, evB), (er, erB) = ex
                    col = 127 if d == 0 else 0
                    S.op("pool", lambda e: e.tensor_copy(out=gam[:, d, cb, g * 4:(g + 1) * 4], in_=ei[:].rearrange("p (j t) -> p j t", t=128)[:, :, col]), reads=[eiB], writes=[self.gamB])
                    yield
                    tt, ttB = f32r.next()
                    S.op("act", lambda e: e.activation(out=tt[:], in_=a[:], func=AF.Identity, scale=pcol("ka"), bias=pv[:, DV["omka"] + cb:DV["omka"] + cb + 1]), reads=[aB], writes=[ttB])
                    yield
                    kt, ktB = f32r.next()
                    ew(lambda e: e.tensor_tensor(out=kt[:], in0=kT[:], in1=tt[:], op=ALU.mult), [kB, ttB], [ktB])
                    yield
                    kts.append((kt, ktB))
                    b, bB = f32r.next()
                    ew(lambda e: e.tensor_tensor(out=b[:], in0=kk[:], in1=a[:], op=ALU.mult), [kkB, aB], [bB])
                    yield
                    outs = []
                    o, oB = b16r.next()
                    S.op("dve", lambda e, o=o: e.scalar_tensor_tensor(out=o[:], in0=kk[:], scalar=-1.0, in1=ee[:], op0=ALU.mult, op1=ALU.mult), reads=[kkB, eeB], writes=[oB])
                    yield
                    outs.append((o, oB))
                    for x, xB, y, yB in [(rT, rB, ei, eiB), (b, bB, ev, evB), (kt, ktB, ev, evB), (b, bB, er, erB), (kt, ktB, er, erB)]:
                        o, oB = b16r.next()
                        ew(lambda e, o=o, x=x, y=y: e.tensor_tensor(out=o[:], in0=x[:], in1=y[:], op=ALU.mult), [xB, yB], [oB])
                        yield
                        outs.append((o, oB))
                    for i in range(4):
                        defer(self.opT[d, i, cs_, gs], outs[i][0][:], reads=[outs[i][1]])
                        yield
                    for i in range(2):
                        to_tm(outs[4 + i][0], outs[4 + i][1], self.optm[d, i].rearrange("(n p) c -> p n c", p=128)[:, g * 4:(g + 1) * 4, cs_])
                        yield
                gens = [dbody(0), dbody(1)]
                while gens:
                    for gq in list(gens):
                        try:
                            next(gq)
                        except StopIteration:
                            gens.remove(gq)
                ks, ksB = f32r.next()
                ew(lambda e: e.tensor_tensor(out=ks[:], in0=kts[0][0][:], in1=kts[1][0][:], op=ALU.add), [kts[0][1], kts[1][1]], [ksB])
                cp, cpB = b16r.next()
                S.op("dve", lambda e: e.scalar_tensor_tensor(out=cp[:], in0=ks[:], scalar=pcol("rk"), in1=rT[:], op0=ALU.mult, op1=ALU.mult), reads=[ksB, rB], writes=[cpB])
                defer(self.cpT[cs_, gs], cp[:], reads=[cpB])
        flush()


KB.phaseC = phaseC
KB.phaseC1 = phaseC1


def phaseC2(self):
    nc, S, T, NT = self.nc, self.S, self.T, self.NT
    self.yT = self.scratch("yT_s", [2, RWC, T], F32)
    gam = self.gam
    NSL = 4
    with ExitStack() as st:
        sb = lambda n, s, dt: st.enter_context(nc.sbuf_tensor(n, list(s), dt))
        ps = lambda n, s, dt: st.enter_context(nc.psum_tensor(n, list(s), dt))
        cB = Buf()
        maskA = [sb("mA%d" % d, [128, 2, 256], F32) for d in range(2)]
        maskN = [sb("mN%d" % d, [128, 2, 128], F32) for d in range(2)]
        id2b = sb("id2b", [128, 2, 128], BF16)
        with ExitStack() as stc:
            cf2, cf2B = self.load_cf(stc, "cf_c2")
            cfs = lambda nm: cf2[:, CT[nm]:CT[nm] + 128]
            for he in range(2):
                for d, (ms, mi, mn) in enumerate([("mus", "mui", "mls"), ("mls", "mli", "mus")]):
                    S.op("dve", lambda e, d=d, ms=ms: e.tensor_copy(out=maskA[d][:, he, 0:128], in_=cfs(ms)), reads=[cf2B], writes=[cB])
                    S.op("dve", lambda e, d=d, mi=mi: e.tensor_copy(out=maskA[d][:, he, 128:256], in_=cfs(mi)), reads=[cf2B], writes=[cB])
                    S.op("dve", lambda e, d=d, mn=mn: e.tensor_copy(out=maskN[d][:, he, :], in_=cfs(mn)), reads=[cf2B], writes=[cB])
                S.op("dve", lambda e: e.tensor_copy(out=id2b[:, he, :], in_=cfs("ident")), reads=[cf2B], writes=[cB])
            S.barrier()
        H = sb("Hst", [128, 2, 8, 128], F32)
        Hb = sb("Hbf", [128, 2, 8, 128], BF16)
        HB = [[Buf() for _ in range(8)] for _ in range(2)]
        S.op("pool", lambda e: e.memset(H[:], 0.0), writes=[b for r in HB for b in r])
        S.op("pool", lambda e: e.memset(Hb[:], 0.0), writes=[b for r in HB for b in r])
        fm = [Ring([tuple([sb("c2f%d_%d_%d" % (d, i, j), [128, 8, 128], BF16) for j in range(3)] +
                          [sb("c2g%d_%d_%d" % (d, i, j), [128, 8, 2, 128], BF16) for j in range(3)]) for i in range(2)]) for d in range(2)]
        pd = [Ring([tuple(sb("c2p%d_%d_%d" % (d, i, j), [128, 8, 2, 128], BF16) for j in range(3)) for i in range(2)]) for d in range(2)]
        for d in range(2):
            for tl, tB in fm[d].items:
                for t in tl[3:]:
                    S.op("pool", lambda e, t=t: e.memset(t[:], 0.0), writes=[tB])
            for tl, tB in pd[d].items:
                for t in tl:
                    S.op("pool", lambda e, t=t: e.memset(t[:], 0.0), writes=[tB])
        ubp = Ring([sb("ubp%d" % i, [128, 2, 128], BF16) for i in range(8)])
        for t, tB in ubp.items:
            S.op("pool", lambda e, t=t: e.memset(t[:], 0.0), writes=[tB])
        ytr = [Ring([sb("yt%d_%d" % (d, i), [128, 8, 128], F32) for i in range(2)]) for d in range(2)]
        bk = [ps("c2b%d" % i, [128, 512], F32) for i in range(8)]
        bkB = [Buf() for _ in range(8)]
        v3 = lambda ap: ap.rearrange("p (a b) -> p a b", a=2)
        fl2 = lambda t: t[:].rearrange("p a b -> p (a b)")
        mk = lambda n, shp, dt, cnt: [Ring([sb("%s%d_%d" % (n, s, i), shp, dt) for i in range(cnt)]) for s in range(NSL)]
        NBr = mk("NB", [128, 2, 256], BF16, 2)
        KBr = mk("KB", [128, 2, 256], BF16, 2)
        Nr = mk("Nc", [128, 2, 128], BF16, 3)
        Ntr = mk("Ntc", [128, 2, 128], BF16, 4)
        Pr = mk("Pc", [128, 2, 128], BF16, 3)
        Xbr = mk("Xb", [128, 128], BF16, 2)
        for i in range(NT):
            if i == NT // 2 and i > 0:
                for d in range(2):
                    for cb in range(8):
                        S.op("dve", lambda e, d=d, cb=cb: e.tensor_scalar(H[:, d, cb, :], H[:, d, cb, :], self.fl[:, 0:1], None, ALU.mult), writes=[HB[d][cb]])
                        S.op("pool", lambda e, d=d, cb=cb: e.tensor_copy(out=Hb[:, d, cb, :], in_=H[:, d, cb, :]), writes=[HB[d][cb]])
            ld = []
            for d in range(2):
                c = i if d == 0 else NT - 1 - i
                tsl = slice(c * 128, (c + 1) * 128)
                (At, Rt, Bt, Atp, Btp, Ktp), fB = fm[d].next()
                for j, t in enumerate((At, Rt, Bt)):
                    S.dma("sp", t[:], self.opT[d, j, :, tsl].rearrange("(cb p) t -> p cb t", p=128), writes=[fB], merge=(j > 0))
                for t, j in ((Atp, 0), (Btp, 2), (Ktp, 3)):
                    sv = self.opT[d, j, :, tsl].rearrange("(cb he k) t -> he k cb t", cb=8, he=2)
                    for he in range(2):
                        S.dma("sp", t[he * 64:(he + 1) * 64, :, he, :], sv[he], writes=[fB], merge=True)
                (Bhp, Khp, Vp), pB = pd[d].next()
                for ti, (t, src) in enumerate(((Bhp, self.optm[d, 0]), (Khp, self.optm[d, 1]), (Vp, self.Vtm))):
                    sv = src[tsl, :].rearrange("t (cb he k) -> t cb he k", cb=8, he=2)
                    for he in range(2):
                        S.dma("sp", t[:, :, he, he * 64:(he + 1) * 64], sv[:, :, he, :], writes=[pB], merge=(ti + he > 0))
                yt, ytB = ytr[d].next()
                ld.append(dict(c=c, At=At, Rt=Rt, Bt=Bt, Atp=Atp, Btp=Btp, Ktp=Ktp, fB=fB, Bhp=Bhp, Khp=Khp, Vp=Vp, pB=pB, yt=yt, ytB=ytB))
            for cbp in range(4):
                inst = []
                for cbi in range(2):
                    for d in range(2):
                        sl = 2 * cbi + d
                        q = dict(ld[d])
                        q.update(d=d, cb=2 * cbp + cbi, sl=sl, X=bk[2 * sl], Y=bk[2 * sl + 1], BX=bkB[2 * sl], BY=bkB[2 * sl + 1])
                        inst.append(q)
                for q in inst:
                    cb, fB = q["cb"], q["fB"]
                    psB, psK = v3(q["X"][:, :]), v3(q["Y"][:, :])
                    for he in range(2):
                        S.op("pe", lambda e: e.matmul(psB[:, he, 0:128], q["Btp"][:, cb, he, :], q["At"][:, cb, :], start=True, stop=True), reads=[fB], writes=[q["BX"]], inc=False)
                        S.op("pe", lambda e: e.matmul(psB[:, he, 128:256], q["Btp"][:, cb, he, :], q["Rt"][:, cb, :], start=True, stop=True), reads=[fB], writes=[q["BX"]], inc=(he == 1))
                        S.op("pe", lambda e: e.matmul(psK[:, he, 0:128], q["Ktp"][:, cb, he, :], q["At"][:, cb, :], start=True, stop=True), reads=[fB], writes=[q["BY"]], inc=False)
                        S.op("pe", lambda e: e.matmul(psK[:, he, 128:256], q["Ktp"][:, cb, he, :], q["Rt"][:, cb, :], start=True, stop=True), reads=[fB], writes=[q["BY"]], inc=(he == 1))
                for q in inst:
                    d, sl = q["d"], q["sl"]
                    NB, NBB = NBr[sl].next()
                    KBs, KBB = KBr[sl].next()
                    S.op("dve", lambda e: e.tensor_tensor(out=fl2(NB), in0=q["X"][:, 0:512], in1=fl2(maskA[d]), op=ALU.mult), reads=[q["BX"], cB], writes=[NBB])
                    S.op("dve", lambda e: e.tensor_tensor(out=fl2(KBs), in0=q["Y"][:, 0:512], in1=fl2(maskA[d]), op=ALU.mult), reads=[q["BY"], cB], writes=[KBB])
                    q.update(NB=NB, NBB=NBB, KB=KBs, KBB=KBB)
                for q in inst:
                    cb, fB = q["cb"], q["fB"]
                    psN = v3(q["Y"][:, 0:256])
                    for he in range(2):
                        S.op("pe", lambda e: e.matmul(psN[:, he, :], q["Atp"][:, cb, he, :], q["Bt"][:, cb, :], start=True, stop=True), reads=[fB], writes=[q["BY"]], inc=(he == 1))
                for q in inst:
                    d, sl = q["d"], q["sl"]
                    Nt0, Nt0B = Ntr[sl].next()
                    P0, P0B = Pr[sl].next()
                    S.op("dve", lambda e: e.tensor_tensor(out=fl2(Nt0), in0=q["Y"][:, 0:256], in1=fl2(maskN[d]), op=ALU.mult), reads=[q["BY"], cB], writes=[Nt0B])
                    for he in range(2):
                        S.op("pool", lambda e: e.tensor_tensor(out=P0[:, he, :], in0=q["NB"][:, he, 0:128], in1=id2b[:, he, :], op=ALU.add), reads=[q["NBB"], cB], writes=[P0B])
                    q.update(N=q["NB"][:, :, 0:128], NB_=q["NBB"], Nt=Nt0[:], NtB=Nt0B, P=P0[:], PB=P0B, pend=None)

                def emit_P_mm(q):
                    Ip, IpB = q["pend"]
                    psP = v3(q["Y"][:, 256:512])
                    for he in range(2):
                        S.op("pe", lambda e: e.matmul(psP[:, he, :], Ip[:, he, :], q["P"][:, he, :], start=True, stop=True), reads=[IpB, q["PB"]], writes=[q["BY"]], inc=(he == 1))

                def emit_P_evac(q):
                    Pn, PnB = Pr[q["sl"]].next()
                    S.op("dve", lambda e: e.tensor_tensor(out=fl2(Pn), in0=q["Y"][:, 256:512], in1=q["P"].rearrange("p a b -> p (a b)"), op=ALU.add), reads=[q["BY"], q["PB"]], writes=[PnB])
                    q["P"], q["PB"] = Pn[:], PnB

                for k in range(1, 7):
                    for q in inst:
                        psA, psC = v3(q["X"][:, 0:256]), v3(q["Y"][:, 0:256])
                        for he in range(2):
                            if k < 6:
                                S.op("pe", lambda e: e.matmul(psA[:, he, :], q["Nt"][:, he, :], q["N"][:, he, :], start=True, stop=True), reads=[q["NtB"], q["NB_"]], writes=[q["BX"]], inc=(he == 1))
                            S.op("pe", lambda e: e.matmul(psC[:, he, :], q["N"][:, he, :], q["Nt"][:, he, :], start=True, stop=True), reads=[q["NtB"], q["NB_"]], writes=[q["BY"]], inc=(he == 1))
                    if k > 1:
                        for q in inst:
                            emit_P_mm(q)
                    for q in inst:
                        sl = q["sl"]
                        Ntn, NtnB = Ntr[sl].next()
                        if q["d"] == 1 and USE_ACT_NT:
                            S.op("act", lambda e: e.activation(out=fl2(Ntn), in_=q["Y"][:, 0:256], func=AF.Copy), reads=[q["BY"]], writes=[NtnB])
                        else:
                            S.op("dve", lambda e: e.tensor_copy(out=fl2(Ntn), in_=q["Y"][:, 0:256]), reads=[q["BY"]], writes=[NtnB])
                        q["npend"] = (Ntn, NtnB)
                        if k < 6:
                            Nn, NnB = Nr[sl].next()
                            S.op("act", lambda e: e.activation(out=fl2(Nn), in_=q["X"][:, 0:256], func=AF.Copy), reads=[q["BX"]], writes=[NnB])
                            q["N"], q["NB_"], q["Nt"], q["NtB"] = Nn[:], NnB, Ntn[:], NtnB
                    if k > 1:
                        for q in inst:
                            emit_P_evac(q)
                    for q in inst:
                        q["pend"] = q["npend"]
                for q in inst:
                    emit_P_mm(q)
                for q in inst:
                    emit_P_evac(q)
                for q in inst:
                    d, cb = q["d"], q["cb"]
                    psX = q["X"][:, 0:128]
                    for he in range(2):
                        hs = slice(he * 64, (he + 1) * 64)
                        S.op("pe", lambda e: e.matmul(psX[:, hs], q["KB"][:, he, 0:128], q["Vp"][:, cb, he, hs], start=True, stop=False), reads=[q["KBB"], q["pB"]], writes=[q["BX"]], inc=False)
                        S.op("pe", lambda e: e.matmul(psX[:, hs], q["At"][:, cb, :], Hb[:, d, cb, hs], start=False, stop=True), reads=[q["fB"], HB[d][cb]], writes=[q["BX"]], inc=(he == 1))
                for q in inst:
                    Xb, XbB = Xbr[q["sl"]].next()
                    S.op("dve", lambda e: e.tensor_copy(out=Xb[:], in_=q["X"][:, 0:128]), reads=[q["BX"]], writes=[XbB])
                    q["Xb"], q["XbB"] = Xb, XbB
                for q in inst:
                    psU = q["X"][:, 128:256]
                    for he in range(2):
                        hs = slice(he * 64, (he + 1) * 64)
                        S.op("pe", lambda e: e.matmul(psU[:, hs], q["P"][:, he, :], q["Xb"][:, hs], start=True, stop=True), reads=[q["PB"], q["XbB"]], writes=[q["BX"]], inc=(he == 1))
                for q in inst:
                    psU = q["X"][:, 128:256]
                    U, UB = ubp.next()
                    S.op("dve", lambda e: e.tensor_copy(out=U[:, 0, 0:64], in_=psU[:, 0:64]), reads=[q["BX"]], writes=[UB])
                    S.op("dve", lambda e: e.tensor_copy(out=U[:, 1, 64:128], in_=psU[:, 64:128]), reads=[q["BX"]], writes=[UB])
                    q["U"], q["UB"] = U, UB
                for q in inst:
                    d, cb = q["d"], q["cb"]
                    psY, psH = q["X"][:, 256:384], q["X"][:, 384:512]
                    U, UB = q["U"], q["UB"]
                    S.op("pe", lambda e: e.matmul(psY, Hb[:, d, cb, :], q["Rt"][:, cb, :], start=True, stop=False), reads=[HB[d][cb], q["fB"]], writes=[q["BX"]], inc=False)
                    for he in range(2):
                        S.op("pe", lambda e: e.matmul(psY, U[:, he, :], q["NB"][:, he, 128:256], start=False, stop=False), reads=[UB, q["NBB"]], writes=[q["BX"]], inc=False)
                    for he in range(2):
                        S.op("pe", lambda e: e.matmul(psY, q["Vp"][:, cb, he, :], q["KB"][:, he, 128:256], start=False, stop=(he == 1)), reads=[q["pB"], q["KBB"]], writes=[q["BX"]], inc=False)
                    for he in range(2):
                        S.op("pe", lambda e: e.matmul(psH, q["Bhp"][:, cb, he, :], U[:, he, :], start=(he == 0), stop=False), reads=[q["pB"], UB], writes=[q["BX"]], inc=False)
                    for he in range(2):
                        S.op("pe", lambda e: e.matmul(psH, q["Khp"][:, cb, he, :], q["Vp"][:, cb, he, :], start=False, stop=(he == 1)), reads=[q["pB"]], writes=[q["BX"]], inc=(he == 1))
                for q in inst:
                    d, cb, c = q["d"], q["cb"], q["c"]
                    psY, psH = q["X"][:, 256:384], q["X"][:, 384:512]
                    S.op("dve", lambda e: e.tensor_copy(out=q["yt"][:, cb, :], in_=psY), reads=[q["BX"]], writes=[q["ytB"]])
                    S.op("pool", lambda e: e.tensor_scalar(H[:, d, cb, :], H[:, d, cb, :], gam[:, d, cb, c:c + 1], None, ALU.mult), reads=[self.gamB], writes=[HB[d][cb]])
                    S.op("dve", lambda e: e.tensor_tensor(out=H[:, d, cb, :], in0=psH, in1=H[:, d, cb, :], op=ALU.add), reads=[q["BX"]], writes=[HB[d][cb]])
                    S.op("pool", lambda e: e.tensor_copy(out=Hb[:, d, cb, :], in_=H[:, d, cb, :]), writes=[HB[d][cb]])
            for d in range(2):
                c, yt, ytB = ld[d]["c"], ld[d]["yt"], ld[d]["ytB"]
                S.dma("pool", self.yT[d, :, c * 128:(c + 1) * 128].rearrange("(cb p) t -> p cb t", p=128), yt[:], reads=[ytB])


def phaseC3(self):
    nc, S, T, NG = self.nc, self.S, self.T, self.NG
    pv = self.pv
    self.orwT = self.scratch("orwT_s", [RWC, T], BF16)
    with ExitStack() as st:
        sb = lambda n, s, dt: st.enter_context(nc.sbuf_tensor(n, list(s), dt))
        ps = lambda n, s, dt: st.enter_context(nc.psum_tensor(n, list(s), dt))
        f32r = Ring([sb("c3f%d" % i, [128, 512], F32) for i in range(24)])
        b16r = Ring([sb("c3b%d" % i, [128, 512], BF16) for i in range(12)])
        psr = Ring([ps("c3p%d" % i, [128, 512], F32) for i in range(6)])
        def cbody(g, cb):
            gs = slice(g * 512, (g + 1) * 512)
            if True:
                cs_ = slice(cb * 128, (cb + 1) * 128)
                pcol = lambda nm: pv[:, PV[nm] + cb:PV[nm] + cb + 1]
                yf, yfB = f32r.next()
                yb, ybB = f32r.next()
                vT, vB = f32r.next()
                cp, cpB = b16r.next()
                gg, ggB = f32r.next()
                S.dma("sp", yf[:], self.yT[0, cs_, gs], writes=[yfB])
                yield
                S.dma("sp", yb[:], self.yT[1, cs_, gs], writes=[ybB])
                yield
                S.dma("sp", vT[:], self.zT[2048 + cb * 128:2048 + (cb + 1) * 128, gs], writes=[vB])
                yield
                S.dma("sp", cp[:], self.cpT[cs_, gs], writes=[cpB])
                yield
                S.dma("sp", gg[:], self.ggT[cs_, gs], writes=[ggB])
                yield
                y, yB = f32r.next()
                S.op("dve", lambda e: e.tensor_tensor(out=y[:], in0=yf[:], in1=yb[:], op=ALU.add), reads=[yfB, ybB], writes=[yB])
                yield
                y16, y16B = b16r.next()
                S.op("act", lambda e: e.activation(out=y16[:], in_=y[:], func=AF.Copy), reads=[yB], writes=[y16B])
                yield
                pm, pmB = psr.next()
                S.op("pe", lambda e: e.matmul(pm[:], self.blk2b, y16[:], start=True, stop=True), reads=[y16B], writes=[pmB])
                yield
                dl, dlB = f32r.next()
                S.op("dve", lambda e: e.scalar_tensor_tensor(out=dl[:], in0=pm[:], scalar=-1.0 / 64, in1=y[:], op0=ALU.mult, op1=ALU.add), reads=[pmB, yB], writes=[dlB])
                yield
                sq, sqB = b16r.next()
                S.op("act", lambda e: e.activation(out=sq[:], in_=dl[:], func=AF.Square), reads=[dlB], writes=[sqB])
                yield
                pvv, pvB = psr.next()
                S.op("pe", lambda e: e.matmul(pvv[:], self.blk2b, sq[:], start=True, stop=True), reads=[sqB], writes=[pvB])
                yield
                rs, rsB = f32r.next()
                S.op("dve", lambda e: e.tensor_scalar(rs[:], pvv[:], 1.0 / 64, 64e-5, ALU.mult, ALU.add), reads=[pvB], writes=[rsB])
                yield
                S.op("act", lambda e: e.sqrt(rs[:], rs[:]), reads=[rsB], writes=[rsB])
                yield
                S.op("dve", lambda e: e.reciprocal(rs[:], rs[:]), reads=[rsB], writes=[rsB])
                yield
                yn, ynB = f32r.next()
                S.op("dve", lambda e: e.tensor_tensor(out=yn[:], in0=dl[:], in1=rs[:], op=ALU.mult), reads=[dlB, rsB], writes=[ynB])
                yield
                S.op("act", lambda e: e.activation(out=yn[:], in_=yn[:], func=AF.Identity, scale=pcol("gnw"), bias=pcol("gnb")), reads=[ynB], writes=[ynB])
                yield
                pc, pcB = psr.next()
                S.op("pe", lambda e: e.matmul(pc[:], self.blk2b, cp[:], start=True, stop=True), reads=[cpB], writes=[pcB])
                yield
                bn, bnB = f32r.next()
                S.op("dve", lambda e: e.tensor_tensor(out=bn[:], in0=pc[:], in1=vT[:], op=ALU.mult), reads=[pcB, vB], writes=[bnB])
                yield
                S.op("pool", lambda e: e.tensor_tensor(out=bn[:], in0=bn[:], in1=yn[:], op=ALU.add), reads=[bnB, ynB], writes=[bnB])
                yield
                o, oB = b16r.next()
                S.op("dve", lambda e: e.tensor_tensor(out=o[:], in0=bn[:], in1=gg[:], op=ALU.mult), reads=[bnB, ggB], writes=[oB])
                yield
                S.dma("pool", self.orwT[cs_, gs], o[:], reads=[oB])
                yield
        its = [(g, cb) for g in range(NG) for cb in range(8)]
        for i0 in range(0, len(its), 2):
            gens = [cbody(*its[i0 + i]) for i in range(2) if i0 + i < len(its)]
            while gens:
                for gq in list(gens):
                    try:
                        next(gq)
                    except StopIteration:
                        gens.remove(gq)


KB.phaseC2 = phaseC2
KB.phaseC3 = phaseC3


_TFULL = 4096


def kernel(**inputs):
    xp = np.asarray(inputs["x_prompt"], np.float32)
    xs_ = np.asarray(inputs["x_sample"], np.float32)
    xs = [xp[b] for b in range(4)] + [xs_[2 * j:2 * j + 2].reshape(_TFULL, D) for j in range(4)]
    stypes = [False] * 4 + [True] * 4
    kb = KB(_TFULL, debug=False)
    nc = kb.build(upto="E2")
    maps = make_in_maps(inputs, xs, stypes, _TFULL)
    res = run_bass_kernel_spmd(nc, maps, core_ids=list(range(8)))
    ys = [np.asarray(res.results[c]["y"], np.float32) for c in range(8)]
    y_prompt = np.stack(ys[:4], 0)
    y_sample = np.concatenate([y.reshape(2, _TFULL // 2, D) for y in ys[4:]], 0)
    return (y_prompt, y_sample)


def phaseD1(self):
    nc, S, T, NG = self.nc, self.S, self.T, self.NG
    self.mT = self.scratch("mT_s", [D, T], BF16)
    with ExitStack() as st:
        sb = lambda n, s, dt: st.enter_context(nc.sbuf_tensor(n, list(s), dt))
        ps = lambda n, s, dt: st.enter_context(nc.psum_tensor(n, list(s), dt))
        acts = []
        for nm, src in (("ona", self.onaT), ("orw", self.orwT)):
            t = sb("d1" + nm, [128, 8, T], BF16)
            tB = Buf()
            S.dma("sp", t[:], src.rearrange("(k p) t -> p k t", p=128), writes=[tB])
            acts.append((t, tB))
        wst = Ring([sb("d1ws%d" % i, [128, 8, 128], F32) for i in range(2)])
        wbf = Ring([sb("d1wb%d" % i, [128, 8, 128], BF16) for i in range(4)])
        pm = Ring([ps("d1p%d" % i, [128, 512], F32) for i in range(6)])
        gr = Ring([sb("d1g%d" % i, [128, 512], F32) for i in range(4)])
        tr = Ring([sb("d1t%d" % i, [128, 512], F32) for i in range(4)])
        orr = Ring([sb("d1o%d" % i, [128, 512], BF16) for i in range(2)])
        Ws = (self.w_br_na.rearrange("(k p) n -> p k n", p=128), self.w_br_rw.rearrange("(k p) n -> p k n", p=128))

        def loadw(j):
            res = []
            for br in range(2):
                w32, w32B = wst.next()
                S.dma("sp", w32[:], Ws[br][:, :, j * 128:(j + 1) * 128], writes=[w32B])
                w, wB = wbf.next()
                S.op("act", lambda e, w=w, w32=w32: e.activation(out=w[:], in_=w32[:], func=AF.Copy), reads=[w32B], writes=[wB])
                res.append((w, wB))
            return res

        nxt = loadw(0)
        for j in range(16):
            cur = nxt
            for g in range(NG):
                gs = slice(g * 512, (g + 1) * 512)
                pss = []
                for br in range(2):
                    p, pB = pm.next()
                    w, wB = cur[br]
                    a, aB = acts[br]
                    for k in range(8):
                        S.op("pe", lambda e, k=k: e.matmul(p[:], w[:, k, :], a[:, k, gs], start=(k == 0), stop=(k == 7)), reads=[wB, aB], writes=[pB], inc=(k == 7))
                    pss.append((p, pB))
                if g == 0 and j + 1 < 16:
                    nxt = loadw(j + 1)
                tmps = []
                for br in range(2):
                    gt, gtB = gr.next()
                    S.dma("sp", gt[:], self.gT[br * D + j * 128:br * D + (j + 1) * 128, gs], writes=[gtB])
                    t, tB = tr.next()
                    S.op("dve", lambda e, t=t, gt=gt, br=br: e.tensor_tensor(out=t[:], in0=pss[br][0][:], in1=gt[:], op=ALU.mult), reads=[pss[br][1], gtB], writes=[tB])
                    tmps.append((t, tB))
                o, oB = orr.next()
                S.op("dve", lambda e: e.tensor_tensor(out=o[:], in0=tmps[0][0][:], in1=tmps[1][0][:], op=ALU.add), reads=[tmps[0][1], tmps[1][1]], writes=[oB])
                S.dma("pool", self.mT[j * 128:(j + 1) * 128, gs], o[:], reads=[oB])


def phaseD2(self):
    nc, S, T, NT = self.nc, self.S, self.T, self.NT
    self.x1 = self.scratch("x1_s", [T, D], F32)
    self.h2T = self.scratch("h2T_s", [D, T], BF16)
    with ExitStack() as st:
        sb = lambda n, s, dt: st.enter_context(nc.sbuf_tensor(n, list(s), dt))
        ps = lambda n, s, dt: st.enter_context(nc.psum_tensor(n, list(s), dt))
        wout = sb("d2w", [128, 16, D], BF16)
        woB = Buf()
        stg = Ring([sb("d2s%d" % i, [128, D], F32) for i in range(2)])
        for k in range(16):
            s32, sB = stg.next()
            S.dma("sp", s32[:], self.w_out[k * 128:(k + 1) * 128, :], writes=[sB])
            if k % 2:
                S.op("act", lambda e, k=k, s32=s32: e.activation(out=wout[:, k, :], in_=s32[:], func=AF.Copy), reads=[sB], writes=[woB])
            else:
                S.op("dve", lambda e, k=k, s32=s32: e.tensor_copy(out=wout[:, k, :], in_=s32[:]), reads=[sB], writes=[woB])
        mtr = Ring([sb("d2m%d" % i, [128, 16, 128], BF16) for i in range(2)])
        xr = Ring([sb("d2x%d" % i, [128, D], F32) for i in range(2)])
        x1r = Ring([sb("d2y%d" % i, [128, D], F32) for i in range(2)])
        h2r = Ring([sb("d2h%d" % i, [128, 16, 128], BF16) for i in range(2)])
        pm = Ring([ps("d2p%d" % i, [128, 512], F32) for i in range(4)])
        R = {"junk": Ring([sb("d2junk", [128, D], BF16)]),
             "ss": Ring([sb("d2ss%d" % i, [128, 4], F32) for i in range(2)]),
             "xs": Ring([sb("d2xs%d" % i, [128, D], BF16) for i in range(2)]),
             "psT": Ring([ps("d2psT%d" % i, [128, 1024], BF16) for i in range(2)])}
        mTv = self.mT.rearrange("(k p) t -> p k t", p=128)
        h2v = self.h2T.rearrange("(c p) t -> p c t", p=128)
        def d2body(i):
            ts_ = slice(i * 128, (i + 1) * 128)
            mt, mtB = mtr.next()
            S.dma("sp", mt[:], mTv[:, :, ts_], writes=[mtB])
            xt, xtB = xr.next()
            S.dma("sp", xt[:], self.x[ts_, :], writes=[xtB])
            x1, x1B = x1r.next()
            yield
            for cg in range(4):
                cs_ = slice(cg * 512, (cg + 1) * 512)
                p, pB = pm.next()
                for k in range(16):
                    S.op("pe", lambda e, k=k: e.matmul(p[:], mt[:, k, :], wout[:, k, cs_], start=(k == 0), stop=(k == 15)), reads=[mtB, woB], writes=[pB], inc=(k == 15))
                yield
                S.op("dve", lambda e: e.tensor_tensor(out=x1[:, cs_], in0=p[:], in1=xt[:, cs_], op=ALU.add), reads=[pB, xtB], writes=[x1B])
                yield
            S.dma("pool", self.x1[ts_, :], x1[:], reads=[x1B])
            h2, h2B = h2r.next()
            yield from self.norm_T(x1[:], x1B, PV["ln2"], lambda c: (h2[:, c, :], h2B), R)
            S.dma("pool", h2v[:, :, ts_], h2[:], reads=[h2B])
            yield

        for i0 in range(0, NT, 2):
            run_rr([d2body(i0 + j) for j in range(2) if i0 + j < NT])


def phaseE1(self):
    nc, S, T, NG = self.nc, self.S, self.T, self.NG
    pv = self.pv
    self.actT = self.scratch("actT_s", [FFN, T], BF16)
    with ExitStack() as st:
        sb = lambda n, s, dt: st.enter_context(nc.sbuf_tensor(n, list(s), dt))
        h2 = sb("e1h", [128, 16, T], BF16)
        hB = [Buf() for _ in range(self.NT)]
        h2v = self.h2T.rearrange("(c p) t -> p c t", p=128)
        for g in range(NG):
            S.dma("sp", h2[:, :, g * 512:(g + 1) * 512], h2v[:, :, g * 512:(g + 1) * 512], writes=hB[g * 4:(g + 1) * 4])
        blocks = []
        for j in range(FFN // 128):
            blocks.append(("val", j * 128, 128, j))
            blocks.append(("gate", FFN + j * 128, 128, j))
        self.ws_proj(st, blocks, self.w_ffn_up, h2, hB, 16, self.postE)


def postE(self, ctx, blk, g, p, pB):
    nc, S, T, NG = self.nc, self.S, self.T, self.NG
    pv = self.pv
    kind, c0, n, j = blk
    if "E" not in ctx:
        sb = ctx["sb"]
        ctx["E"] = {"ubv": (sb("e1ubv", [128, T + 2], F32), Buf()), "ubg": (sb("e1ubg", [128, T + 2], F32), Buf()),
                    "cv": sb("e1cv", [128, T], F32),
                    "cg": Ring([sb("e1cg%d" % i, [128, 512], F32) for i in range(2)]),
                    "ao": Ring([sb("e1ao%d" % i, [128, 512], BF16) for i in range(2)])}
        for z, zB in (ctx["E"]["ubv"], ctx["E"]["ubg"]):
            S.op("pool", lambda e, z=z: e.memset(z[:, 0:1], 0.0), writes=[zB])
            S.op("pool", lambda e, z=z: e.memset(z[:, T + 1:T + 2], 0.0), writes=[zB])
    E = ctx["E"]
    col = j if kind == "val" else 44 + j
    pc = lambda nm, tab=PV: pv[:, tab[nm] + col:tab[nm] + col + 1]
    z, zB0 = E["ubv"] if kind == "val" else E["ubg"]
    if ("ezg" + kind) not in ctx:
        ctx["ezg" + kind] = [Buf() for _ in range(NG)]
        ctx["cvB"] = ctx.get("cvB") or [Buf() for _ in range(NG)]
    zgB = ctx["ezg" + kind]
    if p is not None:
        S.op("act", lambda e: e.activation(out=z[:, 1 + g * 512:1 + (g + 1) * 512], in_=p[:], func=AF.Copy), reads=[pB, zB0], writes=[zgB[g]])
    gm = g - 1
    if gm < 0:
        return
    args = (pc("cw1"), pc("cw0"), pc("cw2"), pc("cw0n", DV), pc("cw2n", DV), pc("cb"))
    gms = slice(gm * 512, (gm + 1) * 512)
    if kind == "val":
        cv = E["cv"]
        self.shift_mix(z, zgB, zB0, gm, 128, *args, (cv[:, gms], ctx["cvB"][gm]), lambda o, oB: None)
    else:
        def sink(o, oB):
            sg, sgB = o, oB
            S.op("act", lambda e: e.activation(out=sg[:], in_=o[:], func=AF.Silu), reads=[oB], writes=[sgB])
            ao, aoB = E["ao"].next()
            S.op("dve", lambda e: e.tensor_tensor(out=ao[:], in0=sg[:], in1=E["cv"][:, gms], op=ALU.mult), reads=[sgB, ctx["cvB"][gm]], writes=[aoB])
            S.dma("pool", self.actT[j * 128:(j + 1) * 128, gms], ao[:], reads=[aoB])
        self.shift_mix(z, zgB, zB0, gm, 128, *args, E["cg"], sink)


def phaseE2(self):
    nc, S, T, NT = self.nc, self.S, self.T, self.NT
    KT = FFN // 128
    with ExitStack() as st:
        sb = lambda n, s, dt: st.enter_context(nc.sbuf_tensor(n, list(s), dt))
        ps = lambda n, s, dt: st.enter_context(nc.psum_tensor(n, list(s), dt))
        wd = sb("e2w", [128, KT, 1024], BF16)
        stg = Ring([sb("e2s%d" % i, [128, 1024], F32) for i in range(3)])
        atr = Ring([sb("e2a%d" % i, [128, KT, 128], BF16) for i in range(2)])
        x1r = Ring([sb("e2x%d" % i, [128, 1024], F32) for i in range(2)])
        yr = Ring([sb("e2y%d" % i, [128, 1024], F32) for i in range(2)])
        pm = Ring([ps("e2p%d" % i, [128, 512], F32) for i in range(4)])
        aTv = self.actT.rearrange("(k p) t -> p k t", p=128)
        for hf in range(2):
            wB = Buf()
            for k in range(KT):
                s32, sB = stg.next()
                S.dma("sp", s32[:], self.w_ffn_down[k * 128:(k + 1) * 128, hf * 1024:(hf + 1) * 1024], writes=[sB])
                if k % 2:
                    S.op("act", lambda e, k=k, s32=s32: e.activation(out=wd[:, k, :], in_=s32[:], func=AF.Copy), reads=[sB], writes=[wB])
                else:
                    S.op("dve", lambda e, k=k, s32=s32: e.tensor_copy(out=wd[:, k, :], in_=s32[:]), reads=[sB], writes=[wB])
            def e2body(i, hf=hf, wB=wB):
                ts_ = slice(i * 128, (i + 1) * 128)
                at, atB = atr.next()
                S.dma("sp", at[:], aTv[:, :, ts_], writes=[atB])
                x1, x1B = x1r.next()
                S.dma("sp", x1[:], self.x1[ts_, hf * 1024:(hf + 1) * 1024], writes=[x1B])
                yo, yB = yr.next()
                yield
                for cg in range(2):
                    cs_ = slice(cg * 512, (cg + 1) * 512)
                    p, pB = pm.next()
                    for k in range(KT):
                        S.op("pe", lambda e, k=k: e.matmul(p[:], at[:, k, :], wd[:, k, cs_], start=(k == 0), stop=(k == KT - 1)), reads=[atB, wB], writes=[pB], inc=(k == KT - 1))
                    yield
                    S.op("dve", lambda e: e.tensor_tensor(out=yo[:, cs_], in0=p[:], in1=x1[:, cs_], op=ALU.add), reads=[pB, x1B], writes=[yB])
                    yield
                S.dma("pool", self.y[ts_, hf * 1024:(hf + 1) * 1024], yo[:], reads=[yB])
                yield

            for i0 in range(0, NT, 2):
                run_rr([e2body(i0 + j) for j in range(2) if i0 + j < NT])


KB.phaseD1 = phaseD1
KB.phaseD2 = phaseD2
KB.phaseE1 = phaseE1
KB.postE = postE
KB.phaseE2 = phaseE2
```

```python
from contextlib import ExitStack
import numpy as np
import ml_dtypes
import concourse.bass as bass
import concourse.mybir as mybir
from concourse.bass_utils import run_bass_kernel_spmd

F32 = mybir.dt.float32
BF16 = mybir.dt.bfloat16
AF = mybir.ActivationFunctionType
ALU = mybir.AluOpType
AX = mybir.AxisListType

D = 2048
NAW = 1024
RWC = 1024
RW_IN = 3488
INW = 10656
FFN = 5632
NEG = -30000.0
CDEC = -0.6065306597126334
NZB = 28
USE_ACT_NT = False

PV = {}
_o = 0
for _n, _c in [("ln1", 16), ("ln2", 16), ("qg", 1), ("kg", 1), ("mu0", NZB), ("mu1", NZB), ("w0", 16), ("a0", 16),
               ("kk", 8), ("ka", 8), ("rk", 8), ("gnw", 8), ("gnb", 8), ("cw0", 88), ("cw1", 88), ("cw2", 88), ("cb", 88)]:
    PV[_n] = _o
    _o += _c
NPV = _o
DV = {}
for _n, _c in [("qgs", 1), ("c0", NZB), ("m0n", NZB), ("m1n", NZB), ("omka", 8), ("cw0n", 88), ("cw2n", 88)]:
    DV[_n] = _o
    _o += _c
NPVT = _o

CT = {"ident": 0, "ones": 128, "blk2": 256, "mus": 384, "mui": 512, "mls": 640, "mli": 768, "seg": 896}
NCONST = 896 + 512


class Buf:
    __slots__ = ("w", "r")

    def __init__(self):
        self.w = {}
        self.r = {}


class Sched:
    def __init__(self, nc, st, ndma=40):
        self.nc = nc
        self.E = {"pe": nc.tensor, "dve": nc.vector, "act": nc.scalar, "pool": nc.gpsimd, "sp": nc.sync}
        self.semh = {}
        self.cnt = {}
        self.waited = {}
        for e in self.E:
            self.semh[e] = st.enter_context(nc.semaphore("s_" + e))
            self.cnt[e] = 0
            self.waited[e] = {}
        self.ndma = ndma
        self.dval = [0] * ndma
        for j in range(ndma):
            self.semh[("d", j)] = st.enter_context(nc.semaphore("d%d" % j))
        self.dpool = {"sp": list(range(0, ndma // 2)), "act": list(range(0, ndma // 2)), "pool": list(range(ndma // 2, ndma))}
        self.dma_i = {"sp": 0, "act": 0, "pool": 0}
        self.nins = 0

    def _deps(self, reads, writes):
        d = {}
        for b in reads:
            for k, v in b.w.items():
                if d.get(k, 0) < v:
                    d[k] = v
        for b in writes:
            for k, v in b.w.items():
                if d.get(k, 0) < v:
                    d[k] = v
            for k, v in b.r.items():
                if d.get(k, 0) < v:
                    d[k] = v
        return d

    def _wait(self, eng, d):
        w = self.waited[eng]
        for k, v in d.items():
            if w.get(k, 0) < v:
                self.E[eng].wait_ge(self.semh[k], v)
                w[k] = v
                self.nins += 1

    def op(self, eng, fn, reads=(), writes=(), inc=True):
        d = self._deps(reads, writes)
        if eng == "pe":
            d.pop("pe", None)
        self._wait(eng, d)
        ins = fn(self.E[eng])
        self.nins += 1
        if inc:
            self.cnt[eng] += 1
            ins.then_inc(self.semh[eng], 1)
            v = self.cnt[eng]
        else:
            v = self.cnt[eng] + 1
        for b in reads:
            b.r[eng] = v
        for b in writes:
            b.w = {eng: v}
            b.r = {}
        return ins

    def dma(self, q, out, in_, reads=(), writes=(), merge=False):
        d = self._deps(reads, writes)
        qk = "sp" if q in ("sp", "act") else "pool"
        pl = self.dpool[qk]
        j = pl[self.dma_i[qk] % len(pl)]
        self.dma_i[qk] += 1
        key = ("d", j)
        if d.get(key, 0) < self.dval[j]:
            d[key] = self.dval[j]
        self._wait(q, d)
        self.dval[j] += 16
        self.E[q].dma_start(out=out, in_=in_).then_inc(self.semh[key], 16)
        self.nins += 1
        v = self.dval[j]
        for b in reads:
            b.r[key] = v
        for b in writes:
            if merge:
                b.w[key] = v
            else:
                b.w = {key: v}
                b.r = {}

    def barrier(self):
        tgt = {e: self.cnt[e] for e in self.E}
        for j in range(self.ndma):
            tgt[("d", j)] = self.dval[j]
        for e in self.E:
            d = {k: v for k, v in tgt.items() if k != e and v > 0}
            self._wait(e, d)


def tile_ranges(T):
    NP = T // 128
    R = T // 64
    res = []
    for p in range(NP):
        need = set()
        for grids in ([(0, R)], [(0, R // 2), (R // 2, R // 2)]):
            for qr in range(2):
                r = 2 * p + qr
                for g0, rows in grids:
                    if g0 <= r < g0 + rows:
                        kh = min(8, rows)
                        rs = int(np.clip(r - g0 - kh // 2, 0, rows - kh)) + g0
                        for kr in range(rs, rs + kh):
                            need.add(kr // 2)
        lo, hi = min(need), max(need)
        res.append((lo, hi - lo + 1))
    return res


def na_slots(T):
    NP = T // 128
    HP = NP // 2
    rng = tile_ranges(T)
    special = sorted(set([0, 1, HP - 2, HP - 1, HP, HP + 1, NP - 2, NP - 1]) & set(range(NP)))
    interior = [p for p in range(NP) if p not in special]
    slot_of = {}
    for i, p in enumerate(special):
        slot_of[p] = i
    nslots = len(special)
    if interior:
        for p in interior:
            assert rng[p] == (p - 2, 5), (p, rng[p])
            slot_of[p] = nslots
        nslots += 1
    ntmax = max(n for _, n in rng)
    return rng, slot_of, nslots, ntmax, special, interior


def build_na_tables(rpb, T, sample_type):
    rng, slot_of, nslots, ntmax, special, interior = na_slots(T)
    R = T // 64
    grids = [(0, R // 2), (R // 2, R // 2)] if sample_type else [(0, R)]
    H = rpb.shape[0]
    tab = np.full((nslots, H, 128, ntmax, 128), NEG, np.float32)
    kc = np.arange(64)
    qc = np.arange(64)
    cs = np.clip(qc - 8, 0, 48)
    colv = (kc[:, None] >= cs[None, :]) & (kc[:, None] < cs[None, :] + 16)
    dxi = np.clip(kc[:, None] - qc[None, :] + 15, 0, 30)
    done = set()
    for p in range(T // 128):
        s = slot_of[p]
        if s in done:
            continue
        done.add(s)
        lo, nt = rng[p]
        for j in range(nt):
            for kr in range(2):
                for qr in range(2):
                    r = 2 * p + qr
                    krow = 2 * (lo + j) + kr
                    ok = False
                    for g0, rows in grids:
                        if g0 <= r < g0 + rows:
                            kh = min(8, rows)
                            rs = int(np.clip(r - g0 - kh // 2, 0, rows - kh)) + g0
                            ok = rs <= krow < rs + kh
                    if not ok:
                        continue
                    dy = krow - r + 7
                    vals = rpb[:, dy][:, dxi]
                    blk = np.where(colv[None], vals, np.float32(NEG))
                    tab[s, :, kr * 64:(kr + 1) * 64, j, qr * 64:(qr + 1) * 64] = blk
    return tab


def make_consts():
    c = np.zeros((128, NCONST), np.float32)
    i = np.arange(128)
    c[:, CT["ident"]:CT["ident"] + 128] = np.eye(128)
    c[:, CT["ones"]:CT["ones"] + 128] = 1.0
    c[:, CT["blk2"]:CT["blk2"] + 128] = (i[:, None] // 64 == i[None, :] // 64)
    c[:, CT["mus"]:CT["mus"] + 128] = (i[None, :] > i[:, None])
    c[:, CT["mui"]:CT["mui"] + 128] = (i[None, :] >= i[:, None])
    c[:, CT["mls"]:CT["mls"] + 128] = (i[None, :] < i[:, None])
    c[:, CT["mli"]:CT["mli"] + 128] = (i[None, :] <= i[:, None])
    seg = np.ones(512, np.float32)
    seg[::128] = 0.0
    c[:, CT["seg"]:CT["seg"] + 512] = seg[None, :]
    return c


def make_pvec(inp):
    def cols(v, n=None):
        v = np.asarray(v, np.float32).reshape(-1)
        nb = (len(v) + 127) // 128
        pad = np.zeros(nb * 128, np.float32)
        pad[:len(v)] = v
        return pad.reshape(nb, 128).T

    parts = [cols(inp["ln1"][0]), cols(inp["ln2"][0]), cols(inp["q_gain"][0]), cols(inp["k_gain"][0]),
             cols(inp["shift_mu"][0, 0]), cols(inp["shift_mu"][0, 1]),
             cols(inp["w0"][0]), cols(inp["a0"][0]), cols(inp["k_k"][0]), cols(inp["k_a"][0]), cols(inp["r_k"][0]),
             cols(inp["gn_w"][0]), cols(inp["gn_b"][0]),
             cols(inp["conv_w"][0, 0]), cols(inp["conv_w"][0, 1]), cols(inp["conv_w"][0, 2]), cols(inp["conv_b"][0])]
    pv = np.concatenate(parts, axis=1)
    assert pv.shape == (128, NPV), pv.shape
    return np.ascontiguousarray(pv)


def run_rr(gens):
    gens = list(gens)
    while gens:
        for gq in list(gens):
            try:
                next(gq)
            except StopIteration:
                gens.remove(gq)


class Ring:
    def __init__(self, items):
        self.items = [(t, Buf()) for t in items]
        self.i = 0

    def next(self):
        it = self.items[self.i % len(self.items)]
        self.i += 1
        return it


class KB:
    def __init__(self, T, debug=False):
        self.T = T
        self.NT = T // 128
        self.NG = T // 512
        self.debug = debug
        self.nc = nc = bass.Bass("TRN2", target_bir_lowering=False)
        self.dbg_outs = []
        di = lambda n, s, dt=F32: nc.dram_tensor(n, list(s), dt, kind="ExternalInput").ap()
        self.x = di("x", [T, D])
        self.flags = di("flags", [128, 4])
        self.pvec = di("pvec", [128, NPV])
        self.consts = di("consts", [128, NCONST])
        rng, slot_of, nslots, ntmax, special, interior = na_slots(T)
        self.na = (rng, slot_of, nslots, ntmax)
        self.natab = di("natab", [nslots, 8, 128, ntmax, 128])
        self.w_in = di("w_in", [D, INW])
        self.w_up = di("w_up", [128, RWC])
        self.a_up = di("a_up", [128, RWC])
        self.g_up = di("g_up", [160, RWC])
        self.w_br_na = di("w_br_na", [NAW, D])
        self.w_br_rw = di("w_br_rw", [RWC, D])
        self.w_out = di("w_out", [D, D])
        self.w_ffn_up = di("w_ffn_up", [D, 2 * FFN])
        self.w_ffn_down = di("w_ffn_down", [FFN, D])
        self.y = nc.dram_tensor("y", [T, D], F32, kind="ExternalOutput").ap()

    def load_cf(self, st, name):
        cf = st.enter_context(self.nc.sbuf_tensor(name, [128, NCONST], F32))
        B = Buf()
        self.S.dma("sp", cf[:], self.consts[:, :], writes=[B])
        return cf, B

    def scratch(self, name, shape, dt):
        kind = "ExternalOutput" if self.debug else "Internal"
        if self.debug:
            self.dbg_outs.append(name)
        return self.nc.dram_tensor(name, list(shape), dt, kind=kind).ap()

    def build(self, upto="E2"):
        nc = self.nc
        T = self.T
        with ExitStack() as st:
            self.st = st
            self.S = S = Sched(nc, st)
            sb = lambda n, s, dt: st.enter_context(nc.sbuf_tensor(n, list(s), dt))
            self.pv = sb("pv", [128, NPVT], F32)
            self.cb = sb("cbf", [128, NCONST], BF16)
            self.fl = sb("fl", [128, 4], F32)
            st_cf = ExitStack()
            self.cf = st_cf.enter_context(nc.sbuf_tensor("cf", [128, NCONST], F32))
            B0 = Buf()
            S.dma("sp", self.pv[:, 0:NPV], self.pvec[:, :], writes=[B0])
            S.dma("sp", self.cf[:], self.consts[:, :], writes=[B0])
            S.dma("sp", self.fl[:], self.flags[:, :], writes=[B0])
            S.barrier()
            pv = self.pv
            V = lambda e: e
            S.op("dve", lambda e: e.tensor_copy(out=self.cb[:], in_=self.cf[:]), reads=[B0], writes=[B0])
            S.op("dve", lambda e: e.tensor_scalar(pv[:, DV["qgs"]:DV["qgs"] + 1], pv[:, PV["qg"]:PV["qg"] + 1], 128.0 ** -0.5, None, ALU.mult), writes=[B0])
            c0 = pv[:, DV["c0"]:DV["c0"] + NZB]
            S.op("dve", lambda e: e.tensor_tensor(out=c0, in0=pv[:, PV["mu0"]:PV["mu0"] + NZB], in1=pv[:, PV["mu1"]:PV["mu1"] + NZB], op=ALU.add), writes=[B0])
            S.op("dve", lambda e: e.tensor_scalar(c0, c0, -1.0, 1.0, ALU.mult, ALU.add), writes=[B0])
            fm1 = self.fl[:, 2:3]
            for a, b, n in [("m0n", "mu0", NZB), ("m1n", "mu1", NZB), ("cw0n", "cw0", 88), ("cw2n", "cw2", 88)]:
                S.op("dve", lambda e, a=a, b=b, n=n: e.tensor_scalar(pv[:, DV[a]:DV[a] + n], pv[:, PV[b]:PV[b] + n], fm1, None, ALU.mult), writes=[B0])
            S.op("dve", lambda e: e.tensor_scalar(pv[:, DV["omka"]:DV["omka"] + 8], pv[:, PV["ka"]:PV["ka"] + 8], -1.0, 1.0, ALU.mult, ALU.add), writes=[B0])
            S.barrier()
            st_cf.close()
            self.cf = None
            self.identb = self.cb[:, CT["ident"]:CT["ident"] + 128]
            self.onesb = self.cb[:, CT["ones"]:CT["ones"] + 128]
            self.blk2b = self.cb[:, CT["blk2"]:CT["blk2"] + 128]
            self.qT = self.scratch("qT_s", [8, 128, T], BF16)
            self.kT = self.scratch("kT_s", [8, 128, T], BF16)
            self.Vna = self.scratch("Vna_s", [T, NAW], BF16)
            self.zT = self.scratch("zT_s", [RW_IN, T], F32)
            self.gT = self.scratch("gT_s", [2 * D, T], F32)
            self.phaseA()
            S.barrier()
            order = ["A", "B", "C1", "C2", "C3", "D1", "D2", "E1", "E2"]
            for ph in order[1:order.index(upto) + 1]:
                getattr(self, "phase" + ph)()
                S.barrier()
            return self.finish()

    def finish(self):
        self.S.barrier()
        return self.nc

    def norm_T(self, src, srcB, gcol0, dst_fn, R):
        S = self.S
        pv = self.pv
        junk, jB = R["junk"].next()
        ss, sB = R["ss"].next()
        S.op("dve", lambda e: e.memset(ss[:], 0.0), writes=[sB])
        yield
        S.op("act", lambda e: e.activation(out=junk[:], in_=src, func=AF.Square, accum_out=ss[:, 0:1]), reads=[srcB], writes=[jB, sB])
        yield
        S.op("dve", lambda e: e.tensor_scalar(ss[:, 1:2], ss[:, 0:1], 1.0 / D, 1e-6, ALU.mult, ALU.add), reads=[sB], writes=[sB])
        yield
        S.op("act", lambda e: e.sqrt(ss[:, 3:4], ss[:, 1:2]), reads=[sB], writes=[sB])
        yield
        S.op("dve", lambda e: e.reciprocal(ss[:, 2:3], ss[:, 3:4]), reads=[sB], writes=[sB])
        yield
        xs, xB = R["xs"].next()
        S.op("act", lambda e: e.activation(out=xs[:], in_=src, func=AF.Copy, scale=ss[:, 2:3]), reads=[srcB, sB], writes=[xB])
        yield
        for hlf in range(2):
            pT, pB = R["psT"].next()
            for c8 in range(8):
                c = hlf * 8 + c8
                S.op("pe", lambda e, c=c, c8=c8: e.transpose(pT[:, c8 * 128:(c8 + 1) * 128], xs[:, c * 128:(c + 1) * 128], self.identb),
                     reads=[xB], writes=[pB], inc=(c8 == 7))
            yield
            for c8 in range(8):
                c = hlf * 8 + c8
                dst, dB = dst_fn(c)
                g = pv[:, gcol0 + c:gcol0 + c + 1]
                if c8 % 2 == 0:
                    S.op("act", lambda e, dst=dst, c8=c8, g=g: e.activation(out=dst, in_=pT[:, c8 * 128:(c8 + 1) * 128], func=AF.Copy, scale=g), reads=[pB], writes=[dB])
                else:
                    S.op("dve", lambda e, dst=dst, c8=c8, g=g: e.tensor_scalar(dst, pT[:, c8 * 128:(c8 + 1) * 128], g, None, ALU.mult), reads=[pB], writes=[dB])
                    yield

    def phaseA(self):
        nc, S, T, NT, NG = self.nc, self.S, self.T, self.NT, self.NG
        pv = self.pv
        with ExitStack() as st:
            sb = lambda n, s, dt: st.enter_context(nc.sbuf_tensor(n, list(s), dt))
            ps = lambda n, s, dt: st.enter_context(nc.psum_tensor(n, list(s), dt))
            hT = sb("hT", [128, 16, T], BF16)
            hB = [Buf() for _ in range(NT)]
            psTr = Ring([ps("psT%d" % i, [128, 1024], BF16) for i in range(2)])
            with ExitStack() as st0:
                sb0 = lambda n, s, dt: st0.enter_context(nc.sbuf_tensor(n, list(s), dt))
                R = {"junk": Ring([sb0("junk", [128, D], BF16)]),
                     "ss": Ring([sb0("ss%d" % i, [128, 4], F32) for i in range(2)]),
                     "xs": Ring([sb0("xs%d" % i, [128, D], BF16) for i in range(2)]),
                     "psT": psTr}
                xt = Ring([sb0("xt%d" % i, [128, D], F32) for i in range(2)])
                def a0body(i):
                    x_t, xB = xt.next()
                    S.dma("sp", x_t[:], self.x[i * 128:(i + 1) * 128, :], writes=[xB])
                    yield
                    yield from self.norm_T(x_t[:], xB, PV["ln1"], lambda c, i=i: (hT[:, c, i * 128:(i + 1) * 128], hB[i]), R)
                for i0 in range(0, NT, 2):
                    run_rr([a0body(i0 + j) for j in range(2) if i0 + j < NT])
                S.barrier()
            blocks = []
            for h in range(8):
                blocks.append(("q", h * 128, 128, h))
            for h in range(8):
                blocks.append(("k", NAW + h * 128, 128, h))
            for h in range(8):
                blocks.append(("v", 2 * NAW + h * 128, 128, h))
            for zb in range(NZB):
                blocks.append(("z", 3 * NAW + zb * 128, min(128, RW_IN - zb * 128), zb))
            for g in range(32):
                blocks.append(("g", 3 * NAW + RW_IN + g * 128, 128, g))
            self.R_A = R
            self.ws_proj(st, blocks, self.w_in, hT, hB, 16, self.postA)

    def ws_proj(self, st, blocks, W, actT, actB, KT, post):
        nc, S, T, NG = self.nc, self.S, self.T, self.NG
        self._nws = getattr(self, "_nws", 0) + 1
        tag = "u%d_" % self._nws
        sb = lambda n, s, dt: st.enter_context(nc.sbuf_tensor(tag + n, list(s), dt))
        ps = lambda n, s, dt: st.enter_context(nc.psum_tensor(tag + n, list(s), dt))
        wst = sb("wst", [128, KT, 128], F32)
        wstB = Buf()
        wbf = Ring([sb("wbf%d" % i, [128, KT, 128], BF16) for i in range(2)])
        pm = Ring([ps("pm%d" % i, [128, 512], F32) for i in range(4)])
        Wv = W.rearrange("(k p) n -> p k n", p=128)
        ctx = {"st": st, "sb": sb, "ps": ps}

        def load(b):
            kind, c0, n, idx = blocks[b]
            S.dma("sp", wst[:, :, 0:n], Wv[:, :, c0:c0 + n], writes=[wstB])

        def cast(b):
            kind, c0, n, idx = blocks[b]
            w, wB = wbf.next()
            h = KT // 2
            S.op("act", lambda e: e.activation(out=w[:, 0:h, 0:n], in_=wst[:, 0:h, 0:n], func=AF.Copy), reads=[wstB], writes=[wB])
            S.op("dve", lambda e: e.tensor_copy(out=w[:, h:KT, 0:n], in_=wst[:, h:KT, 0:n]), reads=[wstB, wB], writes=[wB])
            return w, wB

        load(0)
        cur = cast(0)
        if len(blocks) > 1:
            load(1)
        for b in range(len(blocks)):
            kind, c0, n, idx = blocks[b]
            w, wB = cur
            nxt = None
            for g in range(NG):
                p, pB = pm.next()
                for k in range(KT):
                    S.op("pe", lambda e, k=k: e.matmul(p[0:n, :], w[:, k, 0:n], actT[:, k, g * 512:(g + 1) * 512], start=(k == 0), stop=(k == KT - 1)),
                         reads=[wB] + actB[g * 4:(g + 1) * 4], writes=[pB], inc=(k == KT - 1))
                post(ctx, blocks[b], g, p, pB)
                if g == min(1, NG - 1) and b + 1 < len(blocks):
                    nxt = cast(b + 1)
                    if b + 2 < len(blocks):
                        load(b + 2)
            post(ctx, blocks[b], NG, None, None)
            cur = nxt

    def postA(self, ctx, blk, g, p, pB):
        nc, S, T, NG = self.nc, self.S, self.T, self.NG
        pv = self.pv
        kind, c0, n, idx = blk
        if "A" not in ctx:
            sb, ps = ctx["sb"], ctx["ps"]
            ctx["A"] = {
                "sq": Ring([sb("sq%d" % i, [128, 512], BF16) for i in range(2)]),
                "pss": Ring([ps("pss%d" % i, [128, 512], F32) for i in range(2)]),
                "rstd": Ring([sb("rstd%d" % i, [128, 512], F32) for i in range(2)]),
                "ob": Ring([sb("ob%d" % i, [128, 512], BF16) for i in range(3)]),
                "vt": Ring([sb("vt%d" % i, [128, 4, 128], BF16) for i in range(2)]),
                "zb": Ring([sb("zb%d" % i, [128, T + 2], F32) for i in range(2)]),
                "of": Ring([sb("of%d" % i, [128, 512], F32) for i in range(2)]),
            }
            for z, zB in ctx["A"]["zb"].items:
                S.op("pool", lambda e, z=z: e.memset(z[:, 0:1], 0.0), writes=[zB])
                S.op("pool", lambda e, z=z: e.memset(z[:, T + 1:T + 2], 0.0), writes=[zB])
        A = ctx["A"]
        gs = slice(g * 512, (g + 1) * 512)
        if kind in ("q", "k"):
            if p is None:
                return
            sq, sqB = A["sq"].next()
            S.op("act", lambda e: e.activation(out=sq[:], in_=p[:], func=AF.Square), reads=[pB], writes=[sqB])
            pss, pssB = A["pss"].next()
            S.op("pe", lambda e: e.matmul(pss[:], self.onesb, sq[:], start=True, stop=True), reads=[sqB], writes=[pssB])
            rs, rsB = A["rstd"].next()
            S.op("dve", lambda e: e.tensor_scalar(rs[:], pss[:], 1.0 / 128, 1e-6, ALU.mult, ALU.add), reads=[pssB], writes=[rsB])
            S.op("act", lambda e: e.sqrt(rs[:], rs[:]), reads=[rsB], writes=[rsB])
            S.op("dve", lambda e: e.reciprocal(rs[:], rs[:]), reads=[rsB], writes=[rsB])
            ob, obB = A["ob"].next()
            gc = pv[:, DV["qgs"]:DV["qgs"] + 1] if kind == "q" else pv[:, PV["kg"]:PV["kg"] + 1]
            S.op("dve", lambda e: e.scalar_tensor_tensor(out=ob[:], in0=p[:], scalar=gc, in1=rs[:], op0=ALU.mult, op1=ALU.mult), reads=[pB, rsB], writes=[obB])
            dst = (self.qT if kind == "q" else self.kT)[idx, :, gs]
            S.dma("pool", dst, ob[:], reads=[obB])
        elif kind == "v":
            if p is None:
                return
            ob, obB = A["ob"].next()
            S.op("act", lambda e: e.activation(out=ob[:], in_=p[:], func=AF.Copy), reads=[pB], writes=[obB])
            pV, pVB = self.R_A["psT"].next()
            for j in range(4):
                S.op("pe", lambda e, j=j: e.transpose(pV[:, j * 128:(j + 1) * 128], ob[:, j * 128:(j + 1) * 128], self.identb), reads=[obB], writes=[pVB], inc=(j == 3))
            vt, vtB = A["vt"].next()
            S.op("dve", lambda e: e.tensor_copy(out=vt[:].rearrange("p a b -> p (a b)"), in_=pV[:, 0:512]), reads=[pVB], writes=[vtB])
            dst = self.Vna.rearrange("(n p) c -> p n c", p=128)[:, g * 4:(g + 1) * 4, idx * 128:(idx + 1) * 128]
            S.dma("pool", dst, vt[:], reads=[vtB])
        elif kind == "g":
            if p is None:
                return
            ob, obB = A["of"].next()
            S.op("act", lambda e: e.activation(out=ob[:], in_=p[:], func=AF.Sigmoid), reads=[pB], writes=[obB])
            S.dma("pool", self.gT[idx * 128:(idx + 1) * 128, gs], ob[:], reads=[obB])
        elif kind == "z":
            if g == 0:
                ctx["zcur"] = A["zb"].next()
                ctx["zgB"] = ctx.setdefault(("zgB", id(ctx["zcur"][1])), [Buf() for _ in range(NG)])
            z, zB0 = ctx["zcur"]
            zgB = ctx["zgB"]
            if p is not None:
                S.op("act", lambda e: e.activation(out=z[0:n, 1 + g * 512:1 + (g + 1) * 512], in_=p[0:n, :], func=AF.Copy), reads=[pB, zB0], writes=[zgB[g]])
            gm = g - 1
            if gm < 0:
                return
            self.shift_mix(z, zgB, zB0, gm, n, pv[:, DV["c0"] + idx:DV["c0"] + idx + 1], pv[:, PV["mu0"] + idx:PV["mu0"] + idx + 1],
                           pv[:, PV["mu1"] + idx:PV["mu1"] + idx + 1], pv[:, DV["m0n"] + idx:DV["m0n"] + idx + 1],
                           pv[:, DV["m1n"] + idx:DV["m1n"] + idx + 1], None, A["of"],
                           lambda o, oB: S.dma("pool", self.zT[idx * 128:idx * 128 + n, gm * 512:(gm + 1) * 512], o[0:n, :], reads=[oB]))

    def shift_mix(self, z, zgB, zB0, gm, n, c_c, c_p, c_n, c_pfix, c_nfix, bias, ring, sink):
        S, T, NG = self.S, self.T, self.NG
        rd = [zgB[i] for i in (gm - 1, gm, gm + 1) if 0 <= i < NG] + [zB0]
        o, oB = ring.next() if not isinstance(ring, tuple) else ring
        a = 1 + gm * 512
        if bias is None:
            S.op("act", lambda e: e.activation(out=o[0:n, :], in_=z[0:n, a:a + 512], func=AF.Copy, scale=c_c[0:n]), reads=rd, writes=[oB])
        else:
            S.op("act", lambda e: e.activation(out=o[0:n, :], in_=z[0:n, a:a + 512], func=AF.Identity, scale=c_c[0:n], bias=bias[0:n]), reads=rd, writes=[oB])
        S.op("dve", lambda e: e.scalar_tensor_tensor(out=o[0:n, :], in0=z[0:n, a - 1:a + 511], scalar=c_p[0:n], in1=o[0:n, :], op0=ALU.mult, op1=ALU.add), reads=rd + [oB], writes=[oB])
        S.op("dve", lambda e: e.scalar_tensor_tensor(out=o[0:n, :], in0=z[0:n, a + 1:a + 513], scalar=c_n[0:n], in1=o[0:n, :], op0=ALU.mult, op1=ALU.add), reads=rd + [oB], writes=[oB])
        hb = T // 2
        if gm * 512 == hb:
            S.op("dve", lambda e: e.scalar_tensor_tensor(out=o[0:n, 0:1], in0=z[0:n, hb:hb + 1], scalar=c_pfix[0:n], in1=o[0:n, 0:1], op0=ALU.mult, op1=ALU.add), reads=rd + [oB], writes=[oB])
        if (gm + 1) * 512 == hb:
            S.op("dve", lambda e: e.scalar_tensor_tensor(out=o[0:n, 511:512], in0=z[0:n, hb + 1:hb + 2], scalar=c_nfix[0:n], in1=o[0:n, 511:512], op0=ALU.mult, op1=ALU.add), reads=rd + [oB], writes=[oB])
        sink(o, oB)


def make_in_maps(inp, xs, sample_types, T):
    pvec = make_pvec(inp)
    consts = make_consts()
    sq = lambda a: np.ascontiguousarray(np.asarray(a, np.float32)[0])
    shared = {"pvec": pvec, "consts": consts, "w_in": sq(inp["w_in"]),
              "w_up": sq(inp["w_up"]).reshape(128, RWC), "a_up": sq(inp["a_up"]).reshape(128, RWC), "g_up": sq(inp["g_up"]),
              "w_br_na": sq(inp["w_br_na"]), "w_br_rw": sq(inp["w_br_rw"]), "w_out": sq(inp["w_out"]),
              "w_ffn_up": sq(inp["w_ffn_up"]), "w_ffn_down": sq(inp["w_ffn_down"])}
    rpb = sq(inp["rpb"])
    tabs = {False: build_na_tables(rpb, T, False), True: build_na_tables(rpb, T, True)}
    maps = []
    for x, stype in zip(xs, sample_types):
        f = 0.0 if stype else 1.0
        flags = np.zeros((128, 4), np.float32)
        flags[:, 0] = f
        flags[:, 1] = 1.0 - f
        flags[:, 2] = f - 1.0
        m = dict(shared)
        m["x"] = np.ascontiguousarray(x, dtype=np.float32)
        m["flags"] = flags
        m["natab"] = tabs[stype]
        maps.append(m)
    return maps


def phaseB(self):
    nc, S, T, NT = self.nc, self.S, self.T, self.NT
    rng, slot_of, nslots, ntmax = self.na
    self.onaT = self.scratch("onaT_s", [NAW, T], BF16)
    with ExitStack() as st:
        sb = lambda n, s, dt: st.enter_context(nc.sbuf_tensor(n, list(s), dt))
        ps = lambda n, s, dt: st.enter_context(nc.psum_tensor(n, list(s), dt))
        qh = Ring([sb("qh%d" % i, [128, T], BF16) for i in range(2)])
        kh = Ring([sb("kh%d" % i, [128, T], BF16) for i in range(2)])
        vh = Ring([sb("vh%d" % i, [128, NT, 128], BF16) for i in range(2)])
        tb = Ring([sb("tb%d" % i, [128, nslots, ntmax * 128], F32) for i in range(2)])
        oh = Ring([sb("oh%d" % i, [128, T], BF16) for i in range(2)])
        psS = Ring([ps("psS%d" % i, [128, 1024], F32) for i in range(2)])
        psO = Ring([ps("psO%d" % i, [128, 512], F32) for i in range(3)])
        epre = Ring([sb("epre%d" % i, [128, ntmax * 128], F32) for i in range(3)])
        ebf = Ring([sb("ebf%d" % i, [128, ntmax * 128], BF16) for i in range(3)])
        rec = Ring([sb("rec%d" % i, [128, 128], F32) for i in range(3)])
        Vv = self.Vna.rearrange("(n p) c -> p n c", p=128)
        for h in range(8):
            q, qB = qh.next()
            k, kB = kh.next()
            v, vB = vh.next()
            tab, tB = tb.next()
            o, oB = oh.next()
            S.dma("sp", q[:], self.qT[h, :, :], writes=[qB])
            S.dma("sp", k[:], self.kT[h, :, :], writes=[kB])
            S.dma("sp", v[:], Vv[:, :, h * 128:(h + 1) * 128], writes=[vB])
            S.dma("sp", tab[:], self.natab[:, h].rearrange("s p j q -> p s (j q)"), writes=[tB])
            def pbody(p):
                lo, nt = rng[p]
                s = slot_of[p]
                pS, pSB = psS.next()
                for j in range(nt):
                    S.op("pe", lambda e, j=j: e.matmul(pS[:, j * 128:(j + 1) * 128], k[:, (lo + j) * 128:(lo + j + 1) * 128], q[:, p * 128:(p + 1) * 128], start=True, stop=True),
                         reads=[kB, qB], writes=[pSB], inc=(j == nt - 1))
                ep, epB = epre.next()
                w = nt * 128
                for a, b in ([(0, min(w, 512))] + ([(512, w)] if w > 512 else [])):
                    S.op("dve", lambda e, a=a, b=b: e.tensor_tensor(out=ep[:, a:b], in0=pS[:, a:b], in1=tab[:, s, a:b], op=ALU.add), reads=[pSB, tB], writes=[epB])
                    yield
                eb, ebB = ebf.next()
                S.op("act", lambda e: e.activation(out=eb[:, 0:w], in_=ep[:, 0:w], func=AF.Exp), reads=[epB], writes=[ebB])
                yield
                pO, pOB = psO.next()
                for j in range(nt):
                    S.op("pe", lambda e, j=j: e.matmul(pO[:, 0:128], v[:, lo + j, :], eb[:, j * 128:(j + 1) * 128], start=(j == 0), stop=(j == nt - 1)),
                         reads=[vB, ebB], writes=[pOB], inc=False)
                for j in range(nt):
                    S.op("pe", lambda e, j=j: e.matmul(pO[:, 128:256], self.onesb, eb[:, j * 128:(j + 1) * 128], start=(j == 0), stop=(j == nt - 1)),
                         reads=[ebB], writes=[pOB], inc=(j == nt - 1))
                rc, rcB = rec.next()
                S.op("dve", lambda e: e.reciprocal(rc[:], pO[:, 128:256]), reads=[pOB], writes=[rcB])
                yield
                S.op("dve", lambda e: e.tensor_tensor(out=o[:, p * 128:(p + 1) * 128], in0=pO[:, 0:128], in1=rc[:], op=ALU.mult), reads=[pOB, rcB], writes=[oB])
                yield
            for p0 in range(0, NT, 2):
                gens = [pbody(p0 + i) for i in range(2) if p0 + i < NT]
                while gens:
                    for gq in list(gens):
                        try:
                            next(gq)
                        except StopIteration:
                            gens.remove(gq)
            S.dma("pool", self.onaT[h * 128:(h + 1) * 128, :], o[:], reads=[oB])


KB.phaseB = phaseB


def phaseC(self):
    self.phaseC1()
    self.S.barrier()
    self.phaseC2()
    self.S.barrier()
    self.phaseC3()


def phaseC1(self):
    nc, S, T, NT, NG = self.nc, self.S, self.T, self.NT, self.NG
    pv = self.pv
    self.opT = self.scratch("opT_s", [2, 4, RWC, T], BF16)
    self.optm = self.scratch("optm_s", [2, 2, T, RWC], BF16)
    self.Vtm = self.scratch("Vtm_s", [T, RWC], BF16)
    self.ggT = self.scratch("ggT_s", [RWC, T], F32)
    self.cpT = self.scratch("cpT_s", [RWC, T], BF16)
    self.gam = self.st.enter_context(nc.sbuf_tensor("gam", [128, 2, 8, NT], F32))
    self.gamB = Buf()
    gam = self.gam
    with ExitStack() as st:
        sb = lambda n, s, dt: st.enter_context(nc.sbuf_tensor(n, list(s), dt))
        ps = lambda n, s, dt: st.enter_context(nc.psum_tensor(n, list(s), dt))
        wB = Buf()
        wl = {}
        stg = sb("c1stg", [128, RWC], F32)
        stgB = Buf()
        for nm, src, rows in [("wup", self.w_up[:, :], 128), ("aup", self.a_up[:, :], 128), ("gupa", self.g_up[0:128, :], 128), ("gupb", self.g_up[128:160, :], 32)]:
            wl[nm] = sb("c1" + nm, [128, RWC], BF16)
            S.dma("sp", stg[0:rows, :], src, writes=[stgB])
            S.op("dve", lambda e, nm=nm, rows=rows: e.tensor_copy(out=wl[nm][0:rows, :], in_=stg[0:rows, :]), reads=[stgB], writes=[wB])
        f32r = Ring([sb("c1f%d" % i, [128, 512], F32) for i in range(44)])
        b16r = Ring([sb("c1b%d" % i, [128, 512], BF16) for i in range(28)])
        b16l = Ring([sb("c1l%d" % i, [128, 512], BF16) for i in range(8)])
        psr = Ring([ps("c1p%d" % i, [128, 512], F32) for i in range(6)])
        psT = Ring([ps("c1t%d" % i, [128, 1024], BF16) for i in range(2)])
        cf1, cf1B = self.load_cf(st, "cf_c1")
        segm = cf1[:, CT["seg"]:CT["seg"] + 512]
        eng_rr = [0]

        deferred = []

        def defer(dst, src, reads=()):
            deferred.append((dst, src, list(reads)))

        def flush():
            for dst, src, rd in deferred:
                S.dma("sp", dst, src, reads=rd)
            del deferred[:]

        def ew(fn, reads, writes):
            eng_rr[0] += 1
            S.op("pool" if eng_rr[0] % 7 in (0, 3) else "dve", fn, reads=reads, writes=writes)

        def ldf(rows0, n, g):
            t, tB = f32r.next()
            S.dma("sp", t[0:n, :], self.zT[rows0:rows0 + n, g * 512:(g + 1) * 512], writes=[tB])
            return t, tB

        def to_tm(src, srcB, dst_ap):
            pT, pTB = psT.next()
            for j in range(4):
                S.op("pe", lambda e, j=j: e.transpose(pT[:, j * 128:(j + 1) * 128], src[:, j * 128:(j + 1) * 128], self.identb), reads=[srcB], writes=[pTB], inc=(j == 3))
            o, oB = b16r.next()
            S.op("act", lambda e: e.activation(out=o[:], in_=pT[:, 0:512], func=AF.Copy), reads=[pTB], writes=[oB])
            defer(dst_ap, o[:].rearrange("p (j c) -> p j c", c=128), reads=[oB])

        for g in range(NG):
            gs = slice(g * 512, (g + 1) * 512)
            dw, dwB = ldf(3072, 128, g)
            da, daB = ldf(3200, 128, g)
            dg1, dg1B = ldf(3328, 128, g)
            dg2, dg2B = ldf(3456, 32, g)
            tdw, tdwB = b16l.next()
            S.op("act", lambda e: e.activation(out=tdw[:], in_=dw[:], func=AF.Tanh), reads=[dwB], writes=[tdwB])
            dab, dabB = b16l.next()
            S.op("dve", lambda e: e.tensor_copy(out=dab[:], in_=da[:]), reads=[daB], writes=[dabB])
            sg1, sg1B = b16l.next()
            S.op("act", lambda e: e.activation(out=sg1[:], in_=dg1[:], func=AF.Sigmoid), reads=[dg1B], writes=[sg1B])
            sg2, sg2B = b16l.next()
            S.op("act", lambda e: e.activation(out=sg2[0:32, :], in_=dg2[0:32, :], func=AF.Sigmoid), reads=[dg2B], writes=[sg2B])
            import os
            LV = int(os.environ.get("C1LV", "9"))
            for cb in range(8):
                if LV < 2:
                    break
                cs_ = slice(cb * 128, (cb + 1) * 128)
                pcol = lambda nm, off=0: pv[:, PV[nm] + off + cb:PV[nm] + off + cb + 1]
                rT, rB = ldf(cb * 128, 128, g)
                kT, kB = ldf(1024 + cb * 128, 128, g)
                vT, vB = ldf(2048 + cb * 128, 128, g)
                flush()
                kkp, kkpB = f32r.next()
                S.op("act", lambda e: e.activation(out=kkp[:], in_=kT[:], func=AF.Copy, scale=pcol("kk")), reads=[kB], writes=[kkpB])
                sq, sqB = b16r.next()
                S.op("act", lambda e: e.activation(out=sq[:], in_=kkp[:], func=AF.Square), reads=[kkpB], writes=[sqB])
                pss, pssB = psr.next()
                S.op("pe", lambda e: e.matmul(pss[:], self.blk2b, sq[:], start=True, stop=True), reads=[sqB], writes=[pssB])
                rn, rnB = f32r.next()
                S.op("dve", lambda e: e.tensor_scalar(rn[:], pss[:], 1e-12, None, ALU.add), reads=[pssB], writes=[rnB])
                S.op("act", lambda e: e.sqrt(rn[:], rn[:]), reads=[rnB], writes=[rnB])
                S.op("dve", lambda e: e.reciprocal(rn[:], rn[:]), reads=[rnB], writes=[rnB])
                kk, kkB = f32r.next()
                ew(lambda e: e.tensor_tensor(out=kk[:], in0=kkp[:], in1=rn[:], op=ALU.mult), [kkpB, rnB], [kkB])
                if LV < 3:
                    continue
                vb, vbB = b16r.next()
                S.op("act", lambda e: e.activation(out=vb[:], in_=vT[:], func=AF.Copy), reads=[vB], writes=[vbB])
                to_tm(vb, vbB, self.Vtm.rearrange("(n p) c -> p n c", p=128)[:, g * 4:(g + 1) * 4, cs_])
                pg, pgB = psr.next()
                S.op("pe", lambda e: e.matmul(pg[:], wl["gupa"][:, cs_], sg1[:], start=True, stop=False), reads=[wB, sg1B], writes=[pgB], inc=False)
                S.op("pe", lambda e: e.matmul(pg[:], wl["gupb"][0:32, cs_], sg2[0:32, :], start=False, stop=True), reads=[wB, sg2B], writes=[pgB])
                gg, ggB = f32r.next()
                S.op("act", lambda e: e.activation(out=gg[:], in_=pg[:], func=AF.Copy), reads=[pgB], writes=[ggB])
                defer(self.ggT[cs_, gs], gg[:], reads=[ggB])
                kts = []
                if LV < 4:
                    continue
                def dbody(d):
                    ds_ = slice(d * 64, (d + 1) * 64)
                    pw, pwB = psr.next()
                    S.op("pe", lambda e: e.matmul(pw[:], wl["wup"][ds_, cs_], tdw[ds_, :], start=True, stop=True), reads=[wB, tdwB], writes=[pwB])
                    yield
                    sig, sigB = f32r.next()
                    S.op("act", lambda e: e.activation(out=sig[:], in_=pw[:], func=AF.Sigmoid, bias=pcol("w0", d * 8)), reads=[pwB], writes=[sigB])
                    yield
                    pa, paB = psr.next()
                    S.op("pe", lambda e: e.matmul(pa[:], wl["aup"][ds_, cs_], dab[ds_, :], start=True, stop=True), reads=[wB, dabB], writes=[paB])
                    yield
                    a, aB = f32r.next()
                    S.op("act", lambda e: e.activation(out=a[:], in_=pa[:], func=AF.Sigmoid, bias=pcol("a0", d * 8)), reads=[paB], writes=[aB])
                    yield
                    cs, csB = f32r.next()
                    S.op("dve", lambda e: e.tensor_tensor_scan(out=cs[:], data0=segm, data1=sig[:], initial=0.0, op0=ALU.mult, op1=ALU.add), reads=[sigB, cf1B], writes=[csB])
                    yield
                    x3n, x3nB = f32r.next()
                    for j in range(4):
                        js = slice(j * 128, (j + 1) * 128)
                        S.op("pool", lambda e, js=js, j=j: e.tensor_scalar(x3n[:, js], cs[:, js], cs[:, j * 128 + 127:j * 128 + 128], None, ALU.subtract), reads=[csB], writes=[x3nB])
                        yield
                    x2, x2B = f32r.next()
                    ew(lambda e: e.tensor_tensor(out=x2[:], in0=cs[:], in1=sig[:], op=ALU.subtract), [csB, sigB], [x2B])
                    yield
                    if d == 0:
                        inc_src, inc_B = cs, csB
                        e_specs = [(cs, csB, CDEC), (x2, x2B, CDEC), (cs, csB, -CDEC), (x3n, x3nB, -CDEC)]
                    else:
                        x4, x4B = f32r.next()
                        ew(lambda e: e.tensor_tensor(out=x4[:], in0=sig[:], in1=x3n[:], op=ALU.subtract), [sigB, x3nB], [x4B])
                        yield
                        e_specs = [(x4, x4B, CDEC), (x3n, x3nB, -CDEC), (x4, x4B, -CDEC), (x2, x2B, CDEC)]
                    ex = []
                    for src, srcB, sc in e_specs:
                        t, tB = f32r.next()
                        S.op("act", lambda e, t=t, src=src, sc=sc: e.activation(out=t[:], in_=src[:], func=AF.Exp, scale=sc), reads=[srcB], writes=[tB])
                        yield
                        ex.append((t, tB))
                    (ei, eiB), (ee, eeB), (ev, evB), (er, erB) = ex
                    col = 127 if d == 0 else 0
                    S.op("pool", lambda e: e.tensor_copy(out=gam[:, d, cb, g * 4:(g + 1) * 4], in_=ei[:].rearrange("p (j t) -> p j t", t=128)[:, :, col]), reads=[eiB], writes=[self.gamB])
                    yield
                    tt, ttB = f32r.next()
                    S.op("act", lambda e: e.activation(out=tt[:], in_=a[:], func=AF.Identity, scale=pcol("ka"), bias=pv[:, DV["omka"] + cb:DV["omka"] + cb + 1]), reads=[aB], writes=[ttB])
                    yield
                    kt, ktB = f32r.next()
                    ew(lambda e: e.tensor_tensor(out=kt[:], in0=kT[:], in1=tt[:], op=ALU.mult), [kB, ttB], [ktB])
                    yield
                    kts.append((kt, ktB))
                    b, bB = f32r.next()
                    ew(lambda e: e.tensor_tensor(out=b[:], in0=kk[:], in1=a[:], op=ALU.mult), [kkB, aB], [bB])
                    yield
                    outs = []
                    o, oB = b16r.next()
                    S.op("dve", lambda e, o=o: e.scalar_tensor_tensor(out=o[:], in0=kk[:], scalar=-1.0, in1=ee[:], op0=ALU.mult, op1=ALU.mult), reads=[kkB, eeB], writes=[oB])
                    yield
                    outs.append((o, oB))
                    for x, xB, y, yB in [(rT, rB, ei, eiB), (b, bB, ev, evB), (kt, ktB, ev, evB), (b, bB, er, erB), (kt, ktB, er, erB)]:
                        o, oB = b16r.next()
                        ew(lambda e, o=o, x=x, y=y: e.tensor_tensor(out=o[:], in0=x[:], in1=y[:], op=ALU.mult), [xB, yB], [oB])
                        yield
                        outs.append((o, oB))
                    for i in range(4):
                        defer(self.opT[d, i, cs_, gs], outs[i][0][:], reads=[outs[i][1]])
                        yield
                    for i in range(2):
                        to_tm(outs[4 + i][0], outs[4 + i][1], self.optm[d, i].rearrange("(n p) c -> p n c", p=128)[:, g * 4:(g + 1) * 4, cs_])
                        yield
                gens = [dbody(0), dbody(1)]
                while gens:
                    for gq in list(gens):
                        try:
                            next(gq)
                        except StopIteration:
                            gens.remove(gq)
                ks, ksB = f32r.next()
                ew(lambda e: e.tensor_tensor(out=ks[:], in0=kts[0][0][:], in1=kts[1][0][:], op=ALU.add), [kts[0][1], kts[1][1]], [ksB])
                cp, cpB = b16r.next()
                S.op("dve", lambda e: e.scalar_tensor_tensor(out=cp[:], in0=ks[:], scalar=pcol("rk"), in1=rT[:], op0=ALU.mult, op1=ALU.mult), reads=[ksB, rB], writes=[cpB])
                defer(self.cpT[cs_, gs], cp[:], reads=[cpB])
        flush()


KB.phaseC = phaseC
KB.phaseC1 = phaseC1


def phaseC2(self):
    nc, S, T, NT = self.nc, self.S, self.T, self.NT
    self.yT = self.scratch("yT_s", [2, RWC, T], F32)
    gam = self.gam
    NSL = 4
    with ExitStack() as st:
        sb = lambda n, s, dt: st.enter_context(nc.sbuf_tensor(n, list(s), dt))
        ps = lambda n, s, dt: st.enter_context(nc.psum_tensor(n, list(s), dt))
        cB = Buf()
        maskA = [sb("mA%d" % d, [128, 2, 256], F32) for d in range(2)]
        maskN = [sb("mN%d" % d, [128, 2, 128], F32) for d in range(2)]
        id2b = sb("id2b", [128, 2, 128], BF16)
        with ExitStack() as stc:
            cf2, cf2B = self.load_cf(stc, "cf_c2")
            cfs = lambda nm: cf2[:, CT[nm]:CT[nm] + 128]
            for he in range(2):
                for d, (ms, mi, mn) in enumerate([("mus", "mui", "mls"), ("mls", "mli", "mus")]):
                    S.op("dve", lambda e, d=d, ms=ms: e.tensor_copy(out=maskA[d][:, he, 0:128], in_=cfs(ms)), reads=[cf2B], writes=[cB])
                    S.op("dve", lambda e, d=d, mi=mi: e.tensor_copy(out=maskA[d][:, he, 128:256], in_=cfs(mi)), reads=[cf2B], writes=[cB])
                    S.op("dve", lambda e, d=d, mn=mn: e.tensor_copy(out=maskN[d][:, he, :], in_=cfs(mn)), reads=[cf2B], writes=[cB])
                S.op("dve", lambda e: e.tensor_copy(out=id2b[:, he, :], in_=cfs("ident")), reads=[cf2B], writes=[cB])
            S.barrier()
        H = sb("Hst", [128, 2, 8, 128], F32)
        Hb = sb("Hbf", [128, 2, 8, 128], BF16)
        HB = [[Buf() for _ in range(8)] for _ in range(2)]
        S.op("pool", lambda e: e.memset(H[:], 0.0), writes=[b for r in HB for b in r])
        S.op("pool", lambda e: e.memset(Hb[:], 0.0), writes=[b for r in HB for b in r])
        fm = [Ring([tuple([sb("c2f%d_%d_%d" % (d, i, j), [128, 8, 128], BF16) for j in range(3)] +
                          [sb("c2g%d_%d_%d" % (d, i, j), [128, 8, 2, 128], BF16) for j in range(3)]) for i in range(2)]) for d in range(2)]
        pd = [Ring([tuple(sb("c2p%d_%d_%d" % (d, i, j), [128, 8, 2, 128], BF16) for j in range(3)) for i in range(2)]) for d in range(2)]
        for d in range(2):
            for tl, tB in fm[d].items:
                for t in tl[3:]:
                    S.op("pool", lambda e, t=t: e.memset(t[:], 0.0), writes=[tB])
            for tl, tB in pd[d].items:
                for t in tl:
                    S.op("pool", lambda e, t=t: e.memset(t[:], 0.0), writes=[tB])
        ubp = Ring([sb("ubp%d" % i, [128, 2, 128], BF16) for i in range(8)])
        for t, tB in ubp.items:
            S.op("pool", lambda e, t=t: e.memset(t[:], 0.0), writes=[tB])
        ytr = [Ring([sb("yt%d_%d" % (d, i), [128, 8, 128], F32) for i in range(2)]) for d in range(2)]
        bk = [ps("c2b%d" % i, [128, 512], F32) for i in range(8)]
        bkB = [Buf() for _ in range(8)]
        v3 = lambda ap: ap.rearrange("p (a b) -> p a b", a=2)
        fl2 = lambda t: t[:].rearrange("p a b -> p (a b)")
        mk = lambda n, shp, dt, cnt: [Ring([sb("%s%d_%d" % (n, s, i), shp, dt) for i in range(cnt)]) for s in range(NSL)]
        NBr = mk("NB", [128, 2, 256], BF16, 2)
        KBr = mk("KB", [128, 2, 256], BF16, 2)
        Nr = mk("Nc", [128, 2, 128], BF16, 3)
        Ntr = mk("Ntc", [128, 2, 128], BF16, 4)
        Pr = mk("Pc", [128, 2, 128], BF16, 3)
        Xbr = mk("Xb", [128, 128], BF16, 2)
        for i in range(NT):
            if i == NT // 2 and i > 0:
                for d in range(2):
                    for cb in range(8):
                        S.op("dve", lambda e, d=d, cb=cb: e.tensor_scalar(H[:, d, cb, :], H[:, d, cb, :], self.fl[:, 0:1], None, ALU.mult), writes=[HB[d][cb]])
                        S.op("pool", lambda e, d=d, cb=cb: e.tensor_copy(out=Hb[:, d, cb, :], in_=H[:, d, cb, :]), writes=[HB[d][cb]])
            ld = []
            for d in range(2):
                c = i if d == 0 else NT - 1 - i
                tsl = slice(c * 128, (c + 1) * 128)
                (At, Rt, Bt, Atp, Btp, Ktp), fB = fm[d].next()
                for j, t in enumerate((At, Rt, Bt)):
                    S.dma("sp", t[:], self.opT[d, j, :, tsl].rearrange("(cb p) t -> p cb t", p=128), writes=[fB], merge=(j > 0))
                for t, j in ((Atp, 0), (Btp, 2), (Ktp, 3)):
                    sv = self.opT[d, j, :, tsl].rearrange("(cb he k) t -> he k cb t", cb=8, he=2)
                    for he in range(2):
                        S.dma("sp", t[he * 64:(he + 1) * 64, :, he, :], sv[he], writes=[fB], merge=True)
                (Bhp, Khp, Vp), pB = pd[d].next()
                for ti, (t, src) in enumerate(((Bhp, self.optm[d, 0]), (Khp, self.optm[d, 1]), (Vp, self.Vtm))):
                    sv = src[tsl, :].rearrange("t (cb he k) -> t cb he k", cb=8, he=2)
                    for he in range(2):
                        S.dma("sp", t[:, :, he, he * 64:(he + 1) * 64], sv[:, :, he, :], writes=[pB], merge=(ti + he > 0))
                yt, ytB = ytr[d].next()
                ld.append(dict(c=c, At=At, Rt=Rt, Bt=Bt, Atp=Atp, Btp=Btp, Ktp=Ktp, fB=fB, Bhp=Bhp, Khp=Khp, Vp=Vp, pB=pB, yt=yt, ytB=ytB))
            for cbp in range(4):
                inst = []
                for cbi in range(2):
                    for d in range(2):
                        sl = 2 * cbi + d
                        q = dict(ld[d])
                        q.update(d=d, cb=2 * cbp + cbi, sl=sl, X=bk[2 * sl], Y=bk[2 * sl + 1], BX=bkB[2 * sl], BY=bkB[2 * sl + 1])
                        inst.append(q)
                for q in inst:
                    cb, fB = q["cb"], q["fB"]
                    psB, psK = v3(q["X"][:, :]), v3(q["Y"][:, :])
                    for he in range(2):
                        S.op("pe", lambda e: e.matmul(psB[:, he, 0:128], q["Btp"][:, cb, he, :], q["At"][:, cb, :], start=True, stop=True), reads=[fB], writes=[q["BX"]], inc=False)
                        S.op("pe", lambda e: e.matmul(psB[:, he, 128:256], q["Btp"][:, cb, he, :], q["Rt"][:, cb, :], start=True, stop=True), reads=[fB], writes=[q["BX"]], inc=(he == 1))
                        S.op("pe", lambda e: e.matmul(psK[:, he, 0:128], q["Ktp"][:, cb, he, :], q["At"][:, cb, :], start=True, stop=True), reads=[fB], writes=[q["BY"]], inc=False)
                        S.op("pe", lambda e: e.matmul(psK[:, he, 128:256], q["Ktp"][:, cb, he, :], q["Rt"][:, cb, :], start=True, stop=True), reads=[fB], writes=[q["BY"]], inc=(he == 1))
                for q in inst:
                    d, sl = q["d"], q["sl"]
                    NB, NBB = NBr[sl].next()
                    KBs, KBB = KBr[sl].next()
                    S.op("dve", lambda e: e.tensor_tensor(out=fl2(NB), in0=q["X"][:, 0:512], in1=fl2(maskA[d]), op=ALU.mult), reads=[q["BX"], cB], writes=[NBB])
                    S.op("dve", lambda e: e.tensor_tensor(out=fl2(KBs), in0=q["Y"][:, 0:512], in1=fl2(maskA[d]), op=ALU.mult), reads=[q["BY"], cB], writes=[KBB])
                    q.update(NB=NB, NBB=NBB, KB=KBs, KBB=KBB)
                for q in inst:
                    cb, fB = q["cb"], q["fB"]
                    psN = v3(q["Y"][:, 0:256])
                    for he in range(2):
                        S.op("pe", lambda e: e.matmul(psN[:, he, :], q["Atp"][:, cb, he, :], q["Bt"][:, cb, :], start=True, stop=True), reads=[fB], writes=[q["BY"]], inc=(he == 1))
                for q in inst:
                    d, sl = q["d"], q["sl"]
                    Nt0, Nt0B = Ntr[sl].next()
                    P0, P0B = Pr[sl].next()
                    S.op("dve", lambda e: e.tensor_tensor(out=fl2(Nt0), in0=q["Y"][:, 0:256], in1=fl2(maskN[d]), op=ALU.mult), reads=[q["BY"], cB], writes=[Nt0B])
                    for he in range(2):
                        S.op("pool", lambda e: e.tensor_tensor(out=P0[:, he, :], in0=q["NB"][:, he, 0:128], in1=id2b[:, he, :], op=ALU.add), reads=[q["NBB"], cB], writes=[P0B])
                    q.update(N=q["NB"][:, :, 0:128], NB_=q["NBB"], Nt=Nt0[:], NtB=Nt0B, P=P0[:], PB=P0B, pend=None)

                def emit_P_mm(q):
                    Ip, IpB = q["pend"]
                    psP = v3(q["Y"][:, 256:512])
                    for he in range(2):
                        S.op("pe", lambda e: e.matmul(psP[:, he, :], Ip[:, he, :], q["P"][:, he, :], start=True, stop=True), reads=[IpB, q["PB"]], writes=[q["BY"]], inc=(he == 1))

                def emit_P_evac(q):
                    Pn, PnB = Pr[q["sl"]].next()
                    S.op("dve", lambda e: e.tensor_tensor(out=fl2(Pn), in0=q["Y"][:, 256:512], in1=q["P"].rearrange("p a b -> p (a b)"), op=ALU.add), reads=[q["BY"], q["PB"]], writes=[PnB])
                    q["P"], q["PB"] = Pn[:], PnB

                for k in range(1, 7):
                    for q in inst:
                        psA, psC = v3(q["X"][:, 0:256]), v3(q["Y"][:, 0:256])
                        for he in range(2):
                            if k < 6:
                                S.op("pe", lambda e: e.matmul(psA[:, he, :], q["Nt"][:, he, :], q["N"][:, he, :], start=True, stop=True), reads=[q["NtB"], q["NB_"]], writes=[q["BX"]], inc=(he == 1))
                            S.op("pe", lambda e: e.matmul(psC[:, he, :], q["N"][:, he, :], q["Nt"][:, he, :], start=True, stop=True), reads=[q["NtB"], q["NB_"]], writes=[q["BY"]], inc=(he == 1))
                    if k > 1:
                        for q in inst:
                            emit_P_mm(q)
                    for q in inst:
                        sl = q["sl"]
                        Ntn, NtnB = Ntr[sl].next()
                        if q["d"] == 1 and USE_ACT_NT:
                            S.op("act", lambda e: e.activation(out=fl2(Ntn), in_=q["Y"][:, 0:256], func=AF.Copy), reads=[q["BY"]], writes=[NtnB])
                        else:
                            S.op("dve", lambda e: e.tensor_copy(out=fl2(Ntn), in_=q["Y"][:, 0:256]), reads=[q["BY"]], writes=[NtnB])
                        q["npend"] = (Ntn, NtnB)
                        if k < 6:
                            Nn, NnB = Nr[sl].next()
                            S.op("act", lambda e: e.activation(out=fl2(Nn), in_=q["X"][:, 0:256], func=AF.Copy), reads=[q["BX"]], writes=[NnB])
                            q["N"], q["NB_"], q["Nt"], q["NtB"] = Nn[:], NnB, Ntn[:], NtnB
                    if k > 1:
                        for q in inst:
                            emit_P_evac(q)
                    for q in inst:
                        q["pend"] = q["npend"]
                for q in inst:
                    emit_P_mm(q)
                for q in inst:
                    emit_P_evac(q)
                for q in inst:
                    d, cb = q["d"], q["cb"]
                    psX = q["X"][:, 0:128]
                    for he in range(2):
                        hs = slice(he * 64, (he + 1) * 64)
                        S.op("pe", lambda e: e.matmul(psX[:, hs], q["KB"][:, he, 0:128], q["Vp"][:, cb, he, hs], start=True, stop=False), reads=[q["KBB"], q["pB"]], writes=[q["BX"]], inc=False)
                        S.op("pe", lambda e: e.matmul(psX[:, hs], q["At"][:, cb, :], Hb[:, d, cb, hs], start=False, stop=True), reads=[q["fB"], HB[d][cb]], writes=[q["BX"]], inc=(he == 1))
                for q in inst:
                    Xb, XbB = Xbr[q["sl"]].next()
                    S.op("act", lambda e: e.activation(out=Xb[:], in_=q["X"][:, 0:128], func=AF.Copy), reads=[q["BX"]], writes=[XbB])
                    q["Xb"], q["XbB"] = Xb, XbB
                for q in inst:
                    psU = q["X"][:, 128:256]
                    for he in range(2):
                        hs = slice(he * 64, (he + 1) * 64)
                        S.op("pe", lambda e: e.matmul(psU[:, hs], q["P"][:, he, :], q["Xb"][:, hs], start=True, stop=True), reads=[q["PB"], q["XbB"]], writes=[q["BX"]], inc=(he == 1))
                for q in inst:
                    psU = q["X"][:, 128:256]
                    U, UB = ubp.next()
                    S.op("act", lambda e: e.activation(out=U[:, 0, 0:64], in_=psU[:, 0:64], func=AF.Copy), reads=[q["BX"]], writes=[UB])
                    S.op("act", lambda e: e.activation(out=U[:, 1, 64:128], in_=psU[:, 64:128], func=AF.Copy), reads=[q["BX"]], writes=[UB])
                    q["U"], q["UB"] = U, UB
                for q in inst:
                    d, cb = q["d"], q["cb"]
                    psY, psH = q["X"][:, 256:384], q["X"][:, 384:512]
                    U, UB = q["U"], q["UB"]
                    S.op("pe", lambda e: e.matmul(psY, Hb[:, d, cb, :], q["Rt"][:, cb, :], start=True, stop=False), reads=[HB[d][cb], q["fB"]], writes=[q["BX"]], inc=False)
                    for he in range(2):
                        S.op("pe", lambda e: e.matmul(psY, U[:, he, :], q["NB"][:, he, 128:256], start=False, stop=False), reads=[UB, q["NBB"]], writes=[q["BX"]], inc=False)
                    for he in range(2):
                        S.op("pe", lambda e: e.matmul(psY, q["Vp"][:, cb, he, :], q["KB"][:, he, 128:256], start=False, stop=(he == 1)), reads=[q["pB"], q["KBB"]], writes=[q["BX"]], inc=False)
                    for he in range(2):
                        S.op("pe", lambda e: e.matmul(psH, q["Bhp"][:, cb, he, :], U[:, he, :], start=(he == 0), stop=False), reads=[q["pB"], UB], writes=[q["BX"]], inc=False)
                    for he in range(2):
                        S.op("pe", lambda e: e.matmul(psH, q["Khp"][:, cb, he, :], q["Vp"][:, cb, he, :], start=False, stop=(he == 1)), reads=[q["pB"]], writes=[q["BX"]], inc=(he == 1))
                for q in inst:
                    d, cb, c = q["d"], q["cb"], q["c"]
                    psY, psH = q["X"][:, 256:384], q["X"][:, 384:512]
                    S.op("dve", lambda e: e.tensor_copy(out=q["yt"][:, cb, :], in_=psY), reads=[q["BX"]], writes=[q["ytB"]])
                    S.op("pool", lambda e: e.tensor_scalar(H[:, d, cb, :], H[:, d, cb, :], gam[:, d, cb, c:c + 1], None, ALU.mult), reads=[self.gamB], writes=[HB[d][cb]])
                    S.op("dve", lambda e: e.tensor_tensor(out=H[:, d, cb, :], in0=psH, in1=H[:, d, cb, :], op=ALU.add), reads=[q["BX"]], writes=[HB[d][cb]])
                    S.op("pool", lambda e: e.tensor_copy(out=Hb[:, d, cb, :], in_=H[:, d, cb, :]), writes=[HB[d][cb]])
            for d in range(2):
                c, yt, ytB = ld[d]["c"], ld[d]["yt"], ld[d]["ytB"]
                S.dma("pool", self.yT[d, :, c * 128:(c + 1) * 128].rearrange("(cb p) t -> p cb t", p=128), yt[:], reads=[ytB])


def phaseC3(self):
    nc, S, T, NG = self.nc, self.S, self.T, self.NG
    pv = self.pv
    self.orwT = self.scratch("orwT_s", [RWC, T], BF16)
    with ExitStack() as st:
        sb = lambda n, s, dt: st.enter_context(nc.sbuf_tensor(n, list(s), dt))
        ps = lambda n, s, dt: st.enter_context(nc.psum_tensor(n, list(s), dt))
        f32r = Ring([sb("c3f%d" % i, [128, 512], F32) for i in range(24)])
        b16r = Ring([sb("c3b%d" % i, [128, 512], BF16) for i in range(12)])
        psr = Ring([ps("c3p%d" % i, [128, 512], F32) for i in range(6)])
        def cbody(g, cb):
            gs = slice(g * 512, (g + 1) * 512)
            if True:
                cs_ = slice(cb * 128, (cb + 1) * 128)
                pcol = lambda nm: pv[:, PV[nm] + cb:PV[nm] + cb + 1]
                yf, yfB = f32r.next()
                yb, ybB = f32r.next()
                vT, vB = f32r.next()
                cp, cpB = b16r.next()
                gg, ggB = f32r.next()
                S.dma("sp", yf[:], self.yT[0, cs_, gs], writes=[yfB])
                yield
                S.dma("sp", yb[:], self.yT[1, cs_, gs], writes=[ybB])
                yield
                S.dma("sp", vT[:], self.zT[2048 + cb * 128:2048 + (cb + 1) * 128, gs], writes=[vB])
                yield
                S.dma("sp", cp[:], self.cpT[cs_, gs], writes=[cpB])
                yield
                S.dma("sp", gg[:], self.ggT[cs_, gs], writes=[ggB])
                yield
                y, yB = f32r.next()
                S.op("dve", lambda e: e.tensor_tensor(out=y[:], in0=yf[:], in1=yb[:], op=ALU.add), reads=[yfB, ybB], writes=[yB])
                yield
                y16, y16B = b16r.next()
                S.op("act", lambda e: e.activation(out=y16[:], in_=y[:], func=AF.Copy), reads=[yB], writes=[y16B])
                yield
                pm, pmB = psr.next()
                S.op("pe", lambda e: e.matmul(pm[:], self.blk2b, y16[:], start=True, stop=True), reads=[y16B], writes=[pmB])
                yield
                dl, dlB = f32r.next()
                S.op("dve", lambda e: e.scalar_tensor_tensor(out=dl[:], in0=pm[:], scalar=-1.0 / 64, in1=y[:], op0=ALU.mult, op1=ALU.add), reads=[pmB, yB], writes=[dlB])
                yield
                sq, sqB = b16r.next()
                S.op("act", lambda e: e.activation(out=sq[:], in_=dl[:], func=AF.Square), reads=[dlB], writes=[sqB])
                yield
                pvv, pvB = psr.next()
                S.op("pe", lambda e: e.matmul(pvv[:], self.blk2b, sq[:], start=True, stop=True), reads=[sqB], writes=[pvB])
                yield
                rs, rsB = f32r.next()
                S.op("dve", lambda e: e.tensor_scalar(rs[:], pvv[:], 1.0 / 64, 64e-5, ALU.mult, ALU.add), reads=[pvB], writes=[rsB])
                yield
                S.op("act", lambda e: e.sqrt(rs[:], rs[:]), reads=[rsB], writes=[rsB])
                yield
                S.op("dve", lambda e: e.reciprocal(rs[:], rs[:]), reads=[rsB], writes=[rsB])
                yield
                yn, ynB = f32r.next()
                S.op("dve", lambda e: e.tensor_tensor(out=yn[:], in0=dl[:], in1=rs[:], op=ALU.mult), reads=[dlB, rsB], writes=[ynB])
                yield
                S.op("act", lambda e: e.activation(out=yn[:], in_=yn[:], func=AF.Identity, scale=pcol("gnw"), bias=pcol("gnb")), reads=[ynB], writes=[ynB])
                yield
                pc, pcB = psr.next()
                S.op("pe", lambda e: e.matmul(pc[:], self.blk2b, cp[:], start=True, stop=True), reads=[cpB], writes=[pcB])
                yield
                bn, bnB = f32r.next()
                S.op("dve", lambda e: e.tensor_tensor(out=bn[:], in0=pc[:], in1=vT[:], op=ALU.mult), reads=[pcB, vB], writes=[bnB])
                yield
                S.op("pool", lambda e: e.tensor_tensor(out=bn[:], in0=bn[:], in1=yn[:], op=ALU.add), reads=[bnB, ynB], writes=[bnB])
                yield
                o, oB = b16r.next()
                S.op("dve", lambda e: e.tensor_tensor(out=o[:], in0=bn[:], in1=gg[:], op=ALU.mult), reads=[bnB, ggB], writes=[oB])
                yield
                S.dma("pool", self.orwT[cs_, gs], o[:], reads=[oB])
                yield
        its = [(g, cb) for g in range(NG) for cb in range(8)]
        for i0 in range(0, len(its), 2):
            gens = [cbody(*its[i0 + i]) for i in range(2) if i0 + i < len(its)]
            while gens:
                for gq in list(gens):
                    try:
                        next(gq)
                    except StopIteration:
                        gens.remove(gq)


KB.phaseC2 = phaseC2
KB.phaseC3 = phaseC3


_TFULL = 4096


def kernel(**inputs):
    xp = np.asarray(inputs["x_prompt"], np.float32)
    xs_ = np.asarray(inputs["x_sample"], np.float32)
    xs = [xp[b] for b in range(4)] + [xs_[2 * j:2 * j + 2].reshape(_TFULL, D) for j in range(4)]
    stypes = [False] * 4 + [True] * 4
    kb = KB(_TFULL, debug=False)
    nc = kb.build(upto="E2")
    maps = make_in_maps(inputs, xs, stypes, _TFULL)
    res = run_bass_kernel_spmd(nc, maps, core_ids=list(range(8)))
    ys = [np.asarray(res.results[c]["y"], np.float32) for c in range(8)]
    y_prompt = np.stack(ys[:4], 0)
    y_sample = np.concatenate([y.reshape(2, _TFULL // 2, D) for y in ys[4:]], 0)
    return (y_prompt, y_sample)


def phaseD1(self):
    nc, S, T, NG = self.nc, self.S, self.T, self.NG
    self.mT = self.scratch("mT_s", [D, T], BF16)
    with ExitStack() as st:
        sb = lambda n, s, dt: st.enter_context(nc.sbuf_tensor(n, list(s), dt))
        ps = lambda n, s, dt: st.enter_context(nc.psum_tensor(n, list(s), dt))
        acts = []
        for nm, src in (("ona", self.onaT), ("orw", self.orwT)):
            t = sb("d1" + nm, [128, 8, T], BF16)
            tB = Buf()
            S.dma("sp", t[:], src.rearrange("(k p) t -> p k t", p=128), writes=[tB])
            acts.append((t, tB))
        wst = Ring([sb("d1ws%d" % i, [128, 8, 128], F32) for i in range(2)])
        wbf = Ring([sb("d1wb%d" % i, [128, 8, 128], BF16) for i in range(4)])
        pm = Ring([ps("d1p%d" % i, [128, 512], F32) for i in range(6)])
        gr = Ring([sb("d1g%d" % i, [128, 512], F32) for i in range(4)])
        tr = Ring([sb("d1t%d" % i, [128, 512], F32) for i in range(4)])
        orr = Ring([sb("d1o%d" % i, [128, 512], BF16) for i in range(2)])
        Ws = (self.w_br_na.rearrange("(k p) n -> p k n", p=128), self.w_br_rw.rearrange("(k p) n -> p k n", p=128))

        def loadw(j):
            res = []
            for br in range(2):
                w32, w32B = wst.next()
                S.dma("sp", w32[:], Ws[br][:, :, j * 128:(j + 1) * 128], writes=[w32B])
                w, wB = wbf.next()
                S.op("act", lambda e, w=w, w32=w32: e.activation(out=w[:], in_=w32[:], func=AF.Copy), reads=[w32B], writes=[wB])
                res.append((w, wB))
            return res

        nxt = loadw(0)
        for j in range(16):
            cur = nxt
            for g in range(NG):
                gs = slice(g * 512, (g + 1) * 512)
                pss = []
                for br in range(2):
                    p, pB = pm.next()
                    w, wB = cur[br]
                    a, aB = acts[br]
                    for k in range(8):
                        S.op("pe", lambda e, k=k: e.matmul(p[:], w[:, k, :], a[:, k, gs], start=(k == 0), stop=(k == 7)), reads=[wB, aB], writes=[pB], inc=(k == 7))
                    pss.append((p, pB))
                if g == 0 and j + 1 < 16:
                    nxt = loadw(j + 1)
                tmps = []
                for br in range(2):
                    gt, gtB = gr.next()
                    S.dma("sp", gt[:], self.gT[br * D + j * 128:br * D + (j + 1) * 128, gs], writes=[gtB])
                    t, tB = tr.next()
                    S.op("dve", lambda e, t=t, gt=gt, br=br: e.tensor_tensor(out=t[:], in0=pss[br][0][:], in1=gt[:], op=ALU.mult), reads=[pss[br][1], gtB], writes=[tB])
                    tmps.append((t, tB))
                o, oB = orr.next()
                S.op("dve", lambda e: e.tensor_tensor(out=o[:], in0=tmps[0][0][:], in1=tmps[1][0][:], op=ALU.add), reads=[tmps[0][1], tmps[1][1]], writes=[oB])
                S.dma("pool", self.mT[j * 128:(j + 1) * 128, gs], o[:], reads=[oB])


def phaseD2(self):
    nc, S, T, NT = self.nc, self.S, self.T, self.NT
    self.x1 = self.scratch("x1_s", [T, D], F32)
    self.h2T = self.scratch("h2T_s", [D, T], BF16)
    with ExitStack() as st:
        sb = lambda n, s, dt: st.enter_context(nc.sbuf_tensor(n, list(s), dt))
        ps = lambda n, s, dt: st.enter_context(nc.psum_tensor(n, list(s), dt))
        wout = sb("d2w", [128, 16, D], BF16)
        woB = Buf()
        stg = Ring([sb("d2s%d" % i, [128, D], F32) for i in range(2)])
        for k in range(16):
            s32, sB = stg.next()
            S.dma("sp", s32[:], self.w_out[k * 128:(k + 1) * 128, :], writes=[sB])
            if k % 2:
                S.op("act", lambda e, k=k, s32=s32: e.activation(out=wout[:, k, :], in_=s32[:], func=AF.Copy), reads=[sB], writes=[woB])
            else:
                S.op("dve", lambda e, k=k, s32=s32: e.tensor_copy(out=wout[:, k, :], in_=s32[:]), reads=[sB], writes=[woB])
        mtr = Ring([sb("d2m%d" % i, [128, 16, 128], BF16) for i in range(2)])
        xr = Ring([sb("d2x%d" % i, [128, D], F32) for i in range(2)])
        x1r = Ring([sb("d2y%d" % i, [128, D], F32) for i in range(2)])
        h2r = Ring([sb("d2h%d" % i, [128, 16, 128], BF16) for i in range(2)])
        pm = Ring([ps("d2p%d" % i, [128, 512], F32) for i in range(4)])
        R = {"junk": Ring([sb("d2junk", [128, D], BF16)]),
             "ss": Ring([sb("d2ss%d" % i, [128, 4], F32) for i in range(2)]),
             "xs": Ring([sb("d2xs%d" % i, [128, D], BF16) for i in range(2)]),
             "psT": Ring([ps("d2psT%d" % i, [128, 1024], BF16) for i in range(2)])}
        mTv = self.mT.rearrange("(k p) t -> p k t", p=128)
        h2v = self.h2T.rearrange("(c p) t -> p c t", p=128)
        def d2body(i):
            ts_ = slice(i * 128, (i + 1) * 128)
            mt, mtB = mtr.next()
            S.dma("sp", mt[:], mTv[:, :, ts_], writes=[mtB])
            xt, xtB = xr.next()
            S.dma("sp", xt[:], self.x[ts_, :], writes=[xtB])
            x1, x1B = x1r.next()
            yield
            for cg in range(4):
                cs_ = slice(cg * 512, (cg + 1) * 512)
                p, pB = pm.next()
                for k in range(16):
                    S.op("pe", lambda e, k=k: e.matmul(p[:], mt[:, k, :], wout[:, k, cs_], start=(k == 0), stop=(k == 15)), reads=[mtB, woB], writes=[pB], inc=(k == 15))
                yield
                S.op("dve", lambda e: e.tensor_tensor(out=x1[:, cs_], in0=p[:], in1=xt[:, cs_], op=ALU.add), reads=[pB, xtB], writes=[x1B])
                yield
            S.dma("pool", self.x1[ts_, :], x1[:], reads=[x1B])
            h2, h2B = h2r.next()
            yield from self.norm_T(x1[:], x1B, PV["ln2"], lambda c: (h2[:, c, :], h2B), R)
            S.dma("pool", h2v[:, :, ts_], h2[:], reads=[h2B])
            yield

        for i0 in range(0, NT, 2):
            run_rr([d2body(i0 + j) for j in range(2) if i0 + j < NT])


def phaseE1(self):
    nc, S, T, NG = self.nc, self.S, self.T, self.NG
    pv = self.pv
    self.actT = self.scratch("actT_s", [FFN, T], BF16)
    with ExitStack() as st:
        sb = lambda n, s, dt: st.enter_context(nc.sbuf_tensor(n, list(s), dt))
        h2 = sb("e1h", [128, 16, T], BF16)
        hB = [Buf() for _ in range(self.NT)]
        h2v = self.h2T.rearrange("(c p) t -> p c t", p=128)
        for g in range(NG):
            S.dma("sp", h2[:, :, g * 512:(g + 1) * 512], h2v[:, :, g * 512:(g + 1) * 512], writes=hB[g * 4:(g + 1) * 4])
        blocks = []
        for j in range(FFN // 128):
            blocks.append(("val", j * 128, 128, j))
            blocks.append(("gate", FFN + j * 128, 128, j))
        self.ws_proj(st, blocks, self.w_ffn_up, h2, hB, 16, self.postE)


def postE(self, ctx, blk, g, p, pB):
    nc, S, T, NG = self.nc, self.S, self.T, self.NG
    pv = self.pv
    kind, c0, n, j = blk
    if "E" not in ctx:
        sb = ctx["sb"]
        ctx["E"] = {"ubv": (sb("e1ubv", [128, T + 2], F32), Buf()), "ubg": (sb("e1ubg", [128, T + 2], F32), Buf()),
                    "cv": sb("e1cv", [128, T], F32),
                    "cg": Ring([sb("e1cg%d" % i, [128, 512], F32) for i in range(2)]),
                    "ao": Ring([sb("e1ao%d" % i, [128, 512], BF16) for i in range(2)])}
        for z, zB in (ctx["E"]["ubv"], ctx["E"]["ubg"]):
            S.op("pool", lambda e, z=z: e.memset(z[:, 0:1], 0.0), writes=[zB])
            S.op("pool", lambda e, z=z: e.memset(z[:, T + 1:T + 2], 0.0), writes=[zB])
    E = ctx["E"]
    col = j if kind == "val" else 44 + j
    pc = lambda nm, tab=PV: pv[:, tab[nm] + col:tab[nm] + col + 1]
    z, zB0 = E["ubv"] if kind == "val" else E["ubg"]
    if ("ezg" + kind) not in ctx:
        ctx["ezg" + kind] = [Buf() for _ in range(NG)]
        ctx["cvB"] = ctx.get("cvB") or [Buf() for _ in range(NG)]
    zgB = ctx["ezg" + kind]
    if p is not None:
        S.op("act", lambda e: e.activation(out=z[:, 1 + g * 512:1 + (g + 1) * 512], in_=p[:], func=AF.Copy), reads=[pB, zB0], writes=[zgB[g]])
    gm = g - 1
    if gm < 0:
        return
    args = (pc("cw1"), pc("cw0"), pc("cw2"), pc("cw0n", DV), pc("cw2n", DV), pc("cb"))
    gms = slice(gm * 512, (gm + 1) * 512)
    if kind == "val":
        cv = E["cv"]
        self.shift_mix(z, zgB, zB0, gm, 128, *args, (cv[:, gms], ctx["cvB"][gm]), lambda o, oB: None)
    else:
        def sink(o, oB):
            sg, sgB = o, oB
            S.op("act", lambda e: e.activation(out=sg[:], in_=o[:], func=AF.Silu), reads=[oB], writes=[sgB])
            ao, aoB = E["ao"].next()
            S.op("dve", lambda e: e.tensor_tensor(out=ao[:], in0=sg[:], in1=E["cv"][:, gms], op=ALU.mult), reads=[sgB, ctx["cvB"][gm]], writes=[aoB])
            S.dma("pool", self.actT[j * 128:(j + 1) * 128, gms], ao[:], reads=[aoB])
        self.shift_mix(z, zgB, zB0, gm, 128, *args, E["cg"], sink)


def phaseE2(self):
    nc, S, T, NT = self.nc, self.S, self.T, self.NT
    KT = FFN // 128
    with ExitStack() as st:
        sb = lambda n, s, dt: st.enter_context(nc.sbuf_tensor(n, list(s), dt))
        ps = lambda n, s, dt: st.enter_context(nc.psum_tensor(n, list(s), dt))
        wd = sb("e2w", [128, KT, 1024], BF16)
        stg = Ring([sb("e2s%d" % i, [128, 1024], F32) for i in range(3)])
        atr = Ring([sb("e2a%d" % i, [128, KT, 128], BF16) for i in range(2)])
        x1r = Ring([sb("e2x%d" % i, [128, 1024], F32) for i in range(2)])
        yr = Ring([sb("e2y%d" % i, [128, 1024], F32) for i in range(2)])
        pm = Ring([ps("e2p%d" % i, [128, 512], F32) for i in range(4)])
        aTv = self.actT.rearrange("(k p) t -> p k t", p=128)
        for hf in range(2):
            wB = Buf()
            for k in range(KT):
                s32, sB = stg.next()
                S.dma("sp", s32[:], self.w_ffn_down[k * 128:(k + 1) * 128, hf * 1024:(hf + 1) * 1024], writes=[sB])
                if k % 2:
                    S.op("act", lambda e, k=k, s32=s32: e.activation(out=wd[:, k, :], in_=s32[:], func=AF.Copy), reads=[sB], writes=[wB])
                else:
                    S.op("dve", lambda e, k=k, s32=s32: e.tensor_copy(out=wd[:, k, :], in_=s32[:]), reads=[sB], writes=[wB])
            def e2body(i, hf=hf, wB=wB):
                ts_ = slice(i * 128, (i + 1) * 128)
                at, atB = atr.next()
                S.dma("sp", at[:], aTv[:, :, ts_], writes=[atB])
                x1, x1B = x1r.next()
                S.dma("sp", x1[:], self.x1[ts_, hf * 1024:(hf + 1) * 1024], writes=[x1B])
                yo, yB = yr.next()
                yield
                for cg in range(2):
                    cs_ = slice(cg * 512, (cg + 1) * 512)
                    p, pB = pm.next()
                    for k in range(KT):
                        S.op("pe", lambda e, k=k: e.matmul(p[:], at[:, k, :], wd[:, k, cs_], start=(k == 0), stop=(k == KT - 1)), reads=[atB, wB], writes=[pB], inc=(k == KT - 1))
                    yield
                    S.op("dve", lambda e: e.tensor_tensor(out=yo[:, cs_], in0=p[:], in1=x1[:, cs_], op=ALU.add), reads=[pB, x1B], writes=[yB])
                    yield
                S.dma("pool", self.y[ts_, hf * 1024:(hf + 1) * 1024], yo[:], reads=[yB])
                yield

            for i0 in range(0, NT, 2):
                run_rr([e2body(i0 + j) for j in range(2) if i0 + j < NT])


KB.phaseD1 = phaseD1
KB.phaseD2 = phaseD2
KB.phaseE1 = phaseE1
KB.postE = postE
KB.phaseE2 = phaseE2
```
